# Optimizing a Trainium2 kernel written in Bass

```python
import jax, jax.numpy as jnp
from jax import lax
import numpy as np

D_MODEL = 4096
BATCH = 2
SEQ = 4096
DEPTH = 2

ATTN_WIDTH = D_MODEL // 2
CONV_CH = D_MODEL - ATTN_WIDTH
HEAD_DIM = 64
N_Q_HEADS = ATTN_WIDTH // HEAD_DIM
GQA_GROUP = 8
N_KV_HEADS = N_Q_HEADS // GQA_GROUP
WINDOW = 128
ROT_DIM = HEAD_DIM // 4
ROPE_THETA = 500000.0
CONV_WIDTH = 31
Q_COLS = N_Q_HEADS * HEAD_DIM
KV_COLS = N_KV_HEADS * HEAD_DIM
IN_COLS = Q_COLS + 2 * KV_COLS + 2 * CONV_CH
FFN_MULT = 256
D_FF = ((8 * D_MODEL + 3 * FFN_MULT - 1) // (3 * FFN_MULT)) * FFN_MULT
N_MOD = 6

kernel_name = 'hymba_swa_sink_conformer_adaln_block'


def rms_norm(x, g, eps=1e-6):
    xf = x.astype(jnp.float32)
    y = xf * lax.rsqrt(jnp.mean(xf * xf, axis=-1, keepdims=True) + eps)
    return (y * g.astype(jnp.float32)).astype(x.dtype)


def layer_norm(x, g, b, eps=1e-5):
    xf = x.astype(jnp.float32)
    mu = jnp.mean(xf, axis=-1, keepdims=True)
    var = jnp.mean(jnp.square(xf - mu), axis=-1, keepdims=True)
    y = (xf - mu) * lax.rsqrt(var + eps)
    return (y * g.astype(jnp.float32) + b.astype(jnp.float32)).astype(x.dtype)


def rotary_tables(positions):
    inv_freq = ROPE_THETA ** (-jnp.arange(0, ROT_DIM, 2, dtype=jnp.float32) / ROT_DIM)
    ang = positions.astype(jnp.float32)[..., None] * inv_freq
    return jnp.cos(ang)[:, :, None, :], jnp.sin(ang)[:, :, None, :]


def apply_partial_rotary(t, cos, sin):
    tf = t.astype(jnp.float32)
    half = ROT_DIM // 2
    t1, t2, rest = tf[..., :half], tf[..., half:ROT_DIM], tf[..., ROT_DIM:]
    out = jnp.concatenate([t1 * cos - t2 * sin, t2 * cos + t1 * sin, rest], axis=-1)
    return out.astype(t.dtype)


def sliding_window_sink_attention(q, k, v, sinks):
    B, S = q.shape[0], q.shape[1]
    nb = S // WINDOW
    qb = q.reshape(B, nb, WINDOW, N_KV_HEADS, GQA_GROUP, HEAD_DIM)

    def band(t):
        tb = t.reshape(B, nb, WINDOW, N_KV_HEADS, HEAD_DIM)
        prev = jnp.pad(tb, ((0, 0), (1, 0), (0, 0), (0, 0), (0, 0)))[:, :-1]
        return jnp.concatenate([prev, tb], axis=2)

    kb, vb = band(k), band(v)
    s = jnp.einsum('bnqhgd,bnkhd->bnhgqk', qb, kb,
                   preferred_element_type=jnp.float32) * (HEAD_DIM ** -0.5)
    qi = jnp.arange(WINDOW)[:, None]
    ki = jnp.arange(2 * WINDOW)[None, :]
    rel = qi + WINDOW - ki
    in_band = (rel >= 0) & (rel < WINDOW)
    blk = jnp.arange(nb)[:, None, None]
    valid = in_band[None] & ((blk > 0) | (ki[None] >= WINDOW))
    s = jnp.where(valid[None, :, None, None], s, -jnp.inf)
    sink = jnp.broadcast_to(
        sinks.astype(jnp.float32).reshape(1, 1, N_KV_HEADS, GQA_GROUP, 1, 1),
        s.shape[:-1] + (1,))
    p = jax.nn.softmax(jnp.concatenate([s, sink], axis=-1), axis=-1)[..., :-1]
    o = jnp.einsum('bnhgqk,bnkhd->bnqhgd', p.astype(v.dtype), vb)
    return o.reshape(B, S, N_Q_HEADS * HEAD_DIM)


def conformer_conv(u, conv_w, conv_b, ln_g, ln_b):
    a, gate = jnp.split(u, 2, axis=-1)
    h = a * jax.nn.sigmoid(gate)
    h = lax.conv_general_dilated(
        h, conv_w.astype(h.dtype), window_strides=(1,),
        padding=[(CONV_WIDTH - 1, 0)],
        dimension_numbers=('NWC', 'WIO', 'NWC'),
        feature_group_count=CONV_CH) + conv_b
    h = layer_norm(h, ln_g, ln_b)
    return jax.nn.silu(h)


def setup_inputs(seed: int = 0) -> dict:
    key = jax.random.key(seed)
    ks = jax.random.split(key, 24)

    def nrm(k, shape, scale):
        return jax.random.normal(k, shape, jnp.float32) * scale

    def gain(k, shape):
        return 1.0 + 0.02 * jax.random.normal(k, shape, jnp.float32)

    offsets = jax.random.randint(ks[2], (BATCH, 1), 0, 2048, dtype=jnp.int32)
    positions = offsets + jnp.arange(SEQ, dtype=jnp.int32)[None, :]
    return {
        'x': nrm(ks[0], (BATCH, SEQ, D_MODEL), 1.0),
        'c': nrm(ks[1], (BATCH, D_MODEL), 1.0),
        'positions': positions,
        'w_ada': nrm(ks[3], (DEPTH, D_MODEL, N_MOD * D_MODEL), 0.5 * D_MODEL ** -0.5),
        'b_ada': nrm(ks[4], (DEPTH, N_MOD * D_MODEL), 0.01),
        'norm1_g': gain(ks[5], (DEPTH, D_MODEL)),
        'w_in': nrm(ks[6], (DEPTH, D_MODEL, IN_COLS), D_MODEL ** -0.5),
        'q_norm_g': gain(ks[7], (DEPTH, HEAD_DIM)),
        'k_norm_g': gain(ks[8], (DEPTH, HEAD_DIM)),
        'sinks': nrm(ks[9], (DEPTH, N_Q_HEADS), 1.0),
        'conv_w': nrm(ks[10], (DEPTH, CONV_WIDTH, 1, CONV_CH), CONV_WIDTH ** -0.5),
        'conv_b': nrm(ks[11], (DEPTH, CONV_CH), 0.02),
        'conv_ln_g': gain(ks[12], (DEPTH, CONV_CH)),
        'conv_ln_b': nrm(ks[13], (DEPTH, CONV_CH), 0.02),
        'attn_out_g': gain(ks[14], (DEPTH, ATTN_WIDTH)),
        'conv_out_g': gain(ks[15], (DEPTH, CONV_CH)),
        'w_out': nrm(ks[16], (DEPTH, D_MODEL, D_MODEL), D_MODEL ** -0.5),
        'norm2_g': gain(ks[17], (DEPTH, D_MODEL)),
        'w_ffn_gate': nrm(ks[18], (DEPTH, D_MODEL, D_FF), D_MODEL ** -0.5),
        'w_ffn_up': nrm(ks[19], (DEPTH, D_MODEL, D_FF), D_MODEL ** -0.5),
        'w_ffn_down': nrm(ks[20], (DEPTH, D_FF, D_MODEL), D_FF ** -0.5),
    }


def reference(x, c, positions, w_ada, b_ada, norm1_g, w_in, q_norm_g, k_norm_g,
              sinks, conv_w, conv_b, conv_ln_g, conv_ln_b, attn_out_g, conv_out_g,
              w_out, norm2_g, w_ffn_gate, w_ffn_up, w_ffn_down):
    B, S = x.shape[0], x.shape[1]
    cos, sin = rotary_tables(positions)
    c_act = jax.nn.silu(c)
    for l in range(DEPTH):
        mod = c_act @ w_ada[l] + b_ada[l]
        sh1, sc1, g1, sh2, sc2, g2 = [m[:, None, :] for m in jnp.split(mod, N_MOD, axis=-1)]

        h = rms_norm(x, norm1_g[l]) * (1.0 + sc1) + sh1
        proj = h @ w_in[l]
        q, k, v, u = jnp.split(proj, [Q_COLS, Q_COLS + KV_COLS, Q_COLS + 2 * KV_COLS], axis=-1)
        q = q.reshape(B, S, N_Q_HEADS, HEAD_DIM)
        k = k.reshape(B, S, N_KV_HEADS, HEAD_DIM)
        v = v.reshape(B, S, N_KV_HEADS, HEAD_DIM)
        q = apply_partial_rotary(rms_norm(q, q_norm_g[l]), cos, sin)
        k = apply_partial_rotary(rms_norm(k, k_norm_g[l]), cos, sin)
        attn = sliding_window_sink_attention(q, k, v, sinks[l])
        conv = conformer_conv(u, conv_w[l], conv_b[l], conv_ln_g[l], conv_ln_b[l])
        mixed = jnp.concatenate([rms_norm(attn, attn_out_g[l]),
                                 rms_norm(conv, conv_out_g[l])], axis=-1)
        x = x + g1 * (mixed @ w_out[l])

        h = rms_norm(x, norm2_g[l]) * (1.0 + sc2) + sh2
        ffn = (jax.nn.silu(h @ w_ffn_gate[l]) * (h @ w_ffn_up[l])) @ w_ffn_down[l]
        x = x + g2 * ffn
    return x
```

```python
import math
import numpy as np
import concourse.bass as bass
import concourse.mybir as mybir
from concourse.bass_utils import run_bass_kernel_spmd

F32 = mybir.dt.float32
BF16 = mybir.dt.bfloat16
I32 = mybir.dt.int32
AF = mybir.ActivationFunctionType
ALU = mybir.AluOpType

D = 4096
NCH = 32
TS = 1280
DFF = 11008
NFF = 86
INC = 6656
LA = [0, 128]
LB = [128, 256]
NCST = 816
NPRM = 850
PI_LO = 3.1415925
C1 = 6.28125
C2 = 2.0 * math.pi - 6.28125
ENGS = ['pe', 'act', 'dve', 'pool', 'sp']


def tiles_from(t0):
    n = TS - t0
    if n == 1280:
        sz = [512, 512, 256]
    elif n == 1152:
        sz = [384, 384, 384]
    else:
        sz = [512, 512]
    out = []
    s = t0
    for z in sz:
        out.append((s, z))
        s += z
    return out


class Sched:
    def __init__(self, nc):
        self.nc = nc
        self.prog = {e: [] for e in ENGS}
        self.semh = {}
        for e in ENGS:
            self.semh[e] = nc.alloc_semaphore(name=f"s_{e}")
        self.cnt = {e: 0 for e in ENGS}
        self.dcnt = {}
        self.seen = {e: {} for e in ENGS}
        self.res = {}

    def dma_sem(self, name):
        k = ('d', name)
        if k not in self.semh:
            self.semh[k] = self.nc.alloc_semaphore(name=f"d_{name}")
            self.dcnt[k] = 0
        return k

    def op(self, e, fn, reads=(), writes=(), signal=True, dsem=None):
        deps = {}

        def need(kv):
            if kv is None:
                return
            k, v = kv
            if deps.get(k, 0) < v:
                deps[k] = v
        for r in reads:
            st = self.res.get(r)
            if st:
                need(st['w'])
        for w in writes:
            st = self.res.get(w)
            if st:
                need(st['w'])
                for k, v in st['r'].items():
                    need((k, v))
        for k, v in deps.items():
            if e == 'pe' and k == 'pe':
                continue
            if self.seen[e].get(k, 0) < v:
                self.prog[e].append(('wait', k, v))
                self.seen[e][k] = v
        if dsem is not None:
            k = self.dma_sem(dsem)
            self.dcnt[k] += 16
            tag = (k, self.dcnt[k])
            self.prog[e].append(('op', fn, k, 16))
        elif signal:
            self.cnt[e] += 1
            tag = (e, self.cnt[e])
            self.prog[e].append(('op', fn, e, 1))
        else:
            tag = (e, self.cnt[e] + 1)
            self.prog[e].append(('op', fn, None, 0))
        for r in reads:
            st = self.res.setdefault(r, {'w': None, 'r': {}})
            if st['r'].get(tag[0], 0) < tag[1]:
                st['r'][tag[0]] = tag[1]
        for w in writes:
            self.res[w] = {'w': tag, 'r': {}}
        return tag

    def wait_all(self, e, tags):
        for k, v in tags:
            if v > 0 and self.seen[e].get(k, 0) < v:
                self.prog[e].append(('wait', k, v))
                self.seen[e][k] = v

    def barrier(self, engines=('pe', 'act', 'dve', 'sp')):
        tags = [(e, self.cnt[e]) for e in engines]
        for k, v in self.dcnt.items():
            if k[1].startswith('w') or k[1].startswith('ob'):
                continue
            tags.append((k, v))
        for e in engines:
            self.wait_all(e, [t for t in tags if t[0] != e or e == 'sp'])

    def emit(self):
        nc = self.nc
        with nc.Block() as block:
            def run(e, engine):
                for item in self.prog[e]:
                    if item[0] == 'wait':
                        engine.wait_ge(self.semh[item[1]], item[2])
                    else:
                        ins = item[1](engine)
                        if item[2] is not None:
                            ins.then_inc(self.semh[item[2]], item[3])

            @block.tensor
            def _(t):
                run('pe', t)

            @block.scalar
            def _(t):
                run('act', t)

            @block.vector
            def _(t):
                run('dve', t)

            @block.gpsimd
            def _(t):
                run('pool', t)

            @block.sync
            def _(t):
                run('sp', t)


class DummySched:
    def __init__(self):
        self.dcnt = {}
        self.prog = {e: [] for e in ENGS}
        self.semh = {}

    def op(self, *a, **k):
        return None

    def barrier(self, *a, **k):
        pass

    def wait_all(self, *a, **k):
        pass

    def dma_sem(self, name):
        self.dcnt[('d', name)] = 0
        return ('d', name)


class Carve:
    def __init__(self, base, limit):
        self.base = base
        self.off = 0
        self.limit = limit

    def f32(self, n):
        ap = self.base[:, self.off:self.off + n]
        self.off += n
        assert self.off <= self.limit, (self.off, self.limit)
        return ap

    def i32(self, n):
        return self.f32(n).bitcast(I32)

    def bf16(self, n):
        w = (n + 1) // 2
        ap = self.base[:, self.off:self.off + w].bitcast(BF16)
        self.off += w
        assert self.off <= self.limit, (self.off, self.limit)
        return ap


class Builder:
    def __init__(self, depth=2, debug=False, stop=None):
        self.depth = depth
        self.debug = debug
        self.stop = stop
        nc = bass.Bass("TRN2", target_bir_lowering=False)
        self.nc = nc

        def din(name, shape, dt):
            return nc.dram_tensor(name, shape, dt, kind="ExternalInput").ap()
        self.xin = din("xin", [TS, D], F32)
        self.pos = din("pos", [1, TS], I32)
        self.valid = din("valid", [1, TS], F32)
        self.cst = din("cst", [128, NCST], F32)
        self.prm = din("prm", [2, 128, NPRM], F32)
        order = ['x0', 'adaln', 'norm1_', 'inproj', 'attn', 'pnorm', 'outproj', 'norm2_', 'ffn', 'final']
        lvl = 99
        if debug and stop is not None and depth == 1:
            lvl = [i for i, o in enumerate(order) if stop.startswith(o)][0]
        self.lvl = lvl
        nl = depth if debug else 2
        self.nl = nl

        def wshape(need, shp):
            return [nl] + (shp if lvl >= need else [128, 128])
        self.w_ada = din("w_ada", wshape(1, [D, 6 * D]), F32)
        self.w_in = din("w_in", wshape(3, [D, INC]), F32)
        self.w_out = din("w_out", wshape(6, [D, D]), F32)
        self.w_g = din("w_ffn_gate", wshape(8, [D, DFF]), F32)
        self.w_u = din("w_ffn_up", wshape(8, [D, DFF]), F32)
        self.w_d = din("w_ffn_down", wshape(8, [DFF, D]), F32)
        self.out = nc.dram_tensor("out", [1024, D], F32, kind="ExternalOutput").ap()
        kw = {"kind": "ExternalOutput"} if debug else {}
        self.xT = nc.dram_tensor("xT", [D, TS], F32, **kw).ap()
        self.qscr = nc.dram_tensor("qscr", [16, 128, TS], BF16, **kw).ap()
        self.mscr = nc.dram_tensor("mscr", [D, 1152], F32, **kw).ap()
        if debug:
            self.dhb = nc.dram_tensor("dhb", [128, NCH, TS], BF16, **kw).ap()
            self.dr2 = nc.dram_tensor("dr2", [128, 15360], F32, **kw).ap()
            self.dmod = nc.dram_tensor("dmod", [128, 192], F32, **kw).ap()

    def unit_list(self):
        units = []
        for l in range(self.depth):
            if self.lvl < 1:
                break
            wa = self.w_ada[l].rearrange("(kt p) c -> p kt c", p=128)
            for u in range(192):
                units.append((wa[:, :, u * 128:(u + 1) * 128], 32))
            if self.lvl < 3:
                break
            wi = self.w_in[l].rearrange("(kt p) c -> p kt c", p=128)
            order = list(range(20)) + [x for i in range(16) for x in (20 + i, 36 + i)]
            for ch in order:
                units.append((wi[:, :, ch * 128:(ch + 1) * 128], 32))
            if self.lvl < 6:
                break
            wo = self.w_out[l].rearrange("(kt p) c -> p kt c", p=128)
            for oc in range(32):
                units.append((wo[:, :, oc * 128:(oc + 1) * 128], 32))
            if self.lvl < 8:
                break
            wg = self.w_g[l].rearrange("(kt p) c -> p kt c", p=128)
            wu = self.w_u[l].rearrange("(kt p) c -> p kt c", p=128)
            wd = self.w_d[l].rearrange("(kt p) c -> p kt c", p=128)
            for (f0, nf) in self.SC:
                for j in range(nf):
                    f = f0 + j
                    units.append((wg[:, :, f * 128:(f + 1) * 128], 32))
                    units.append((wu[:, :, f * 128:(f + 1) * 128], 32))
                for oc in range(32):
                    units.append((wd[:, f0:f0 + nf, oc * 128:(oc + 1) * 128], nf))
        return units

    SC = [(0, 16), (16, 16), (32, 16), (48, 16), (64, 16), (80, 6)]
    NSLOT = 3
    NOB = 3

    def acquire(self, src, nkt):
        S = self.S
        if self.dry:
            self.units.append((src, nkt))
            return 0
        self.flush_stores()
        self.wcur += 1
        u = self.wcur
        while self.wnext < len(self.units) and self.wnext <= u + self.NSLOT - 1:
            i = self.wnext
            slot = i % self.NSLOT
            src_, nkt_ = self.units[i]
            if nkt_ == 32:
                for h in range(2):
                    S.op('pool', lambda e, slot=slot, src_=src_, h=h: e.dma_start(
                        out=self.WR[:, slot, h * 16:(h + 1) * 16, :], in_=src_[:, h * 16:(h + 1) * 16, :]),
                        writes=[('w', slot, h)], dsem=f"w{slot}_{h}")
            else:
                S.op('pool', lambda e, slot=slot, src_=src_, nkt_=nkt_: e.dma_start(
                    out=self.WR[:, slot, 0:nkt_, :], in_=src_[:, 0:nkt_, :]),
                    writes=[('w', slot, 0), ('w', slot, 1)], dsem=f"w{slot}_0")
            self.wnext += 1
        return u % self.NSLOT

    def ada_job(self, l, u):
        S, ps = self.S, self.ps
        wa = self.w_ada[l].rearrange("(kt p) c -> p kt c", p=128)
        slot = self.acquire(wa[:, :, u * 128:(u + 1) * 128], 32)
        ci = self.adai % 2
        self.adai += 1
        bk = 6 + ci
        for kt in range(32):
            S.op('pe', lambda e, slot=slot, kt=kt, bk=bk: e.matmul(
                ps[:, bk, 0:1], lhsT=self.WR[:, slot, kt, :], rhs=self.CTB[:, kt:kt + 1],
                start=(kt == 0), stop=(kt == 31)),
                reads=[('w', slot, kt // 16), 'ctb'], writes=[('ps', bk)], signal=(kt == 31))
        S.op('act', lambda e, bk=bk, l=l, u=u: e.activation(
            out=self.MODS[:, l, u:u + 1], in_=ps[:, bk, 0:1], func=AF.Identity, bias=self.PRM[:, l, u:u + 1]),
            reads=[('ps', bk), 'prm'], writes=[('mod', l, u // 32)])

    def ada_tick(self, n, upto):
        while n > 0 and self.ada_q and self.ada_q[0] < upto:
            l, u = self.ada_q.pop(0)
            self.ada_job(l, u)
            n -= 1

    def ada_until(self, upto):
        self.ada_tick(10 ** 9, upto)

    def flush_stores(self):
        for fn in self.pending:
            fn()
        self.pending = []

    def x_accum(self, obslot, oc, t0):
        n = TS - t0

        def fn():
            self.S.op('pool', lambda e: e.dma_start(
                out=self.xT[oc * 128:(oc + 1) * 128, t0:TS], in_=self.OB[:, obslot, 0:n], accum_op=ALU.add),
                reads=[('ob', obslot)], writes=[('x', oc)], dsem=f"ob{obslot}")
        self.pending.append(fn)

    def main_mm(self, slot, nkt, pset, tl, rhs_fn, rkeys_fn):
        S = self.S
        for kt in range(nkt):
            for ti, (ts, tn) in enumerate(tl):
                last = (kt == nkt - 1) and (ti == len(tl) - 1)
                b = pset * 3 + ti
                S.op('pe', lambda e, kt=kt, ti=ti, b=b, tn=tn: e.matmul(
                    self.ps[:, b, 0:tn], lhsT=self.WR[:, slot, kt, :], rhs=rhs_fn(kt, ti),
                    start=(kt == 0), stop=(kt == nkt - 1)),
                    reads=[('w', slot, (kt // 16) if nkt == 32 else 0)] + rkeys_fn(kt, ti),
                    writes=[('ps', b)], signal=last)

    def build(self):
        nc = self.nc
        with (
            nc.sbuf_tensor("HB", [128, NCH, TS], BF16) as HB,
            nc.sbuf_tensor("R2", [128, 15360], F32) as R2,
            nc.sbuf_tensor("WR", [128, self.NSLOT, 32, 128], BF16) as WR,
            nc.sbuf_tensor("OB", [128, self.NOB, 1152], F32) as OB,
            nc.sbuf_tensor("CST", [128, NCST], F32) as CST,
            nc.sbuf_tensor("PRM", [128, 2, NPRM], F32) as PRM,
            nc.sbuf_tensor("MODS", [128, 2, 192], F32) as MOD,
            nc.sbuf_tensor("GSS", [128, 2, 2, 32], F32) as GS,
            nc.sbuf_tensor("ONES", [128, 128], F32) as ONES,
            nc.sbuf_tensor("PSWB", [128, 128], BF16) as PSWB,
            nc.sbuf_tensor("CTB", [128, 32], BF16) as CTB,
            nc.sbuf_tensor("QOT", [128, 2, TS], BF16) as QOT,
            nc.psum_tensor("ps", [128, 8, 512], F32) as ps,
        ):
            self.HB, self.R2, self.WR, self.OB, self.CST, self.PRM = HB, R2, WR, OB, CST, PRM
            self.MODS, self.GSS, self.ONES, self.PSWB, self.CTB, self.ps = MOD, GS, ONES, PSWB, CTB, ps
            self.QOT = QOT
            self.KVOFF = 15360 - 3840
            kv = Carve(R2, 15360)
            kv.off = self.KVOFF
            self.KT = kv.bf16(2 * TS).rearrange("p (c t) -> p c t", c=2)
            self.KS = kv.bf16(2 * TS).rearrange("p (c t) -> p c t", c=2)
            self.VT = kv.bf16(10 * 256).rearrange("p (b c) -> p b c", b=10)

            def program():
                S = self.S
                self.wcur = -1
                self.wnext = 0
                self.pending = []
                self.obi = 0
                self.adai = 0
                self.ada_q = [(l, u) for l in range(self.depth) for u in range(192)] if self.lvl >= 1 else []
                S.op('sp', lambda e: e.dma_start(out=CST[:, :], in_=self.cst[:, :]), writes=['cst'], dsem='cst')
                S.op('sp', lambda e: e.dma_start(out=PRM[:, :, :], in_=self.prm.rearrange("l p n -> p l n")), writes=['prm'], dsem='prm')
                S.op('dve', lambda e: e.memset(ONES[:, :], 1.0), writes=['ones'])
                S.op('dve', lambda e: e.tensor_copy(out=PSWB[:, :], in_=CST[:, 384:512]), reads=['cst'], writes=['pswb'])
                S.op('act', lambda e: e.activation(out=CTB[:, :], in_=CST[:, 778:810], func=AF.Silu), reads=['cst'], writes=['ctb'])
                plist = [('x0', lambda: self.phase_x0())]
                for l in range(self.depth):
                    plist += [(f'adaln{l}', lambda l=l: self.phase_adaln(l)),
                              (f'norm1_{l}', lambda l=l: self.phase_norm(l, 1)),
                              (f'inproj{l}', lambda l=l: self.phase_inproj(l)),
                              (f'attn{l}', lambda l=l: self.phase_attn(l)),
                              (f'pnorm{l}', lambda l=l: self.phase_pnorm(l)),
                              (f'outproj{l}', lambda l=l: self.phase_outproj(l)),
                              (f'norm2_{l}', lambda l=l: self.phase_norm(l, 2)),
                              (f'ffn{l}', lambda l=l: self.phase_ffn(l))]
                plist.append(('final', lambda: self.phase_final()))
                for name, fn in plist:
                    S.barrier()
                    fn()
                    if self.stop is not None and self.stop.startswith(name):
                        break
                self.flush_stores()
                S.barrier()
                if self.debug:
                    S.op('sp', lambda e: e.dma_start(out=self.dhb[:, :, :], in_=HB[:, :, :]), dsem='dbg1')
                    S.op('sp', lambda e: e.dma_start(out=self.dr2[:, :], in_=R2[:, :]), dsem='dbg2')
                    S.op('sp', lambda e: e.dma_start(out=self.dmod[:, :], in_=MOD[:, 0, :]), dsem='dbg3')
                S.wait_all('sp', [(k, v) for k, v in S.dcnt.items()])

            self.units = []
            self.dry = True
            self.S = DummySched()
            program()
            self.dry = False
            self.S = S = Sched(nc)
            program()
            S.emit()
        return nc

    def phase_x0(self):
        S, ps, CST = self.S, self.ps, self.CST
        cv = Carve(self.R2, 15360)
        XB = [cv.f32(4096) for _ in range(2)]
        XS = cv.f32(4096).rearrange("p (c t) -> p c t", c=32)
        xTv = self.xT.rearrange("(c p) t -> p c t", p=128)
        BM = [0, 1, 2, 3, 4, 5, 0, 1]
        for tb in range(10):
            self.ada_tick(7, (0, 64))
            xb = XB[tb % 2]
            S.op('sp', lambda e, xb=xb, tb=tb: e.dma_start(out=xb, in_=self.xin[tb * 128:(tb + 1) * 128, :]),
                 writes=[('xb', tb % 2)], dsem=f"xb{tb % 2}")
            for c in range(32):
                bk = BM[c // 4]
                S.op('pe', lambda e, xb=xb, c=c, bk=bk: e.transpose(
                    ps[:, bk, (c % 4) * 128:(c % 4 + 1) * 128], xb[:, c * 128:(c + 1) * 128], CST[:, 0:128]),
                    reads=[('xb', tb % 2), 'cst'], writes=[('ps', bk)], signal=(c % 4 == 3))
                if c % 4 == 3:
                    b = c // 4
                    eng = 'act' if bk % 2 == 0 else 'dve'
                    o_ = XS[:, b * 4:(b + 1) * 4, :]
                    i_ = ps[:, bk, :].rearrange("p (j t) -> p j t", j=4)
                    if eng == 'act':
                        S.op('act', lambda e, o_=o_, i_=i_: e.activation(out=o_, in_=i_, func=AF.Copy),
                             reads=[('ps', bk)], writes=[('xs', b)])
                    else:
                        S.op('dve', lambda e, o_=o_, i_=i_: e.tensor_copy(out=o_, in_=i_),
                             reads=[('ps', bk)], writes=[('xs', b)])
            S.op('sp', lambda e, tb=tb: e.dma_start(out=xTv[:, :, tb * 128:(tb + 1) * 128], in_=XS[:, :, :]),
                 reads=[('xs', b) for b in range(8)], writes=[('x', c) for c in range(32)], dsem="xsst")

    def phase_adaln(self, l):
        self.ada_until((l, 64) if l == 0 else (l, 192))
        self.gs_compute(l, 0)

    def gs_compute(self, l, which):
        S = self.S
        a, b_ = (32, 192) if which == 0 else (128, 224)
        S.op('dve', lambda e: e.scalar_tensor_tensor(out=self.GSS[:, l, which, :], in0=self.MODS[:, l, a:a + 32], scalar=1.0,
                                                     in1=self.PRM[:, l, b_:b_ + 32], op0=ALU.add, op1=ALU.mult),
             reads=[('mod', l, a // 32), 'prm'], writes=[('gs', l, which)])

    def phase_norm(self, l, which):
        S, ps, HB, CST = self.S, self.ps, self.HB, self.CST
        t0 = LA[l] if which == 1 else LB[l]
        gi = 0 if which == 1 else 1
        shoff = 0 if which == 1 else 96
        if which == 2:
            self.ada_until((l, 160))
            self.gs_compute(l, 1)
        MODl = self.MODS[:, l, :]
        GSl = self.GSS[:, l, :, :]
        cv = Carve(self.R2, 15360)
        XT = cv.f32(32 * 384).rearrange("p (c t) -> p c t", c=32)
        SQ = [cv.f32(384) for _ in range(2)]
        TM = [cv.f32(384) for _ in range(2)]
        RT = cv.f32(384)
        RS = cv.f32(384)
        xv = self.xT.rearrange("(c p) t -> p c t", p=128)
        ntl = []
        s_ = t0
        while s_ < TS:
            n_ = min(384, TS - s_)
            ntl.append((s_, n_))
            s_ += n_
        for (ts, tn) in ntl:
            for q in range(4):
                S.op('sp', lambda e, q=q, ts=ts, tn=tn: e.dma_start(out=XT[:, 8 * q:8 * q + 8, 0:tn], in_=xv[:, 8 * q:8 * q + 8, ts:ts + tn]),
                     reads=[('x', c) for c in range(8 * q, 8 * q + 8)], writes=[('nx', q)], dsem=f"nx{q}")
            for c in range(32):
                sq = SQ[c % 2]
                S.op('act', lambda e, c=c, sq=sq, tn=tn: e.activation(out=sq[:, 0:tn], in_=XT[:, c, 0:tn], func=AF.Square),
                     reads=[('nx', c // 8)], writes=[('nsq', c % 2)])
                S.op('pe', lambda e, sq=sq, tn=tn, c=c: e.matmul(ps[:, 6, 0:tn], lhsT=self.ONES[:, :], rhs=sq[:, 0:tn],
                                                                 start=(c == 0), stop=(c == 31)),
                     reads=[('nsq', c % 2), 'ones'], writes=[('ps', 6)])
            S.op('act', lambda e, tn=tn: e.activation(out=RT[:, 0:tn], in_=ps[:, 6, 0:tn], func=AF.Sqrt,
                                                      bias=CST[:, 811:812], scale=1.0 / D),
                 reads=[('ps', 6), 'cst'], writes=['nrt'])
            S.op('dve', lambda e, tn=tn: e.reciprocal(out=RS[:, 0:tn], in_=RT[:, 0:tn]), reads=['nrt'], writes=['nrs'])
            for c in range(32):
                tm = TM[c % 2]
                S.op('dve', lambda e, c=c, tm=tm, tn=tn: e.tensor_tensor(out=tm[:, 0:tn], in0=XT[:, c, 0:tn], in1=RS[:, 0:tn], op=ALU.mult),
                     reads=[('nx', c // 8), 'nrs'], writes=[('ntm', c % 2)])
                S.op('act', lambda e, tm=tm, c=c, ts=ts, tn=tn: e.activation(
                    out=HB[:, c, ts - t0:ts - t0 + tn], in_=tm[:, 0:tn], func=AF.Identity,
                    bias=MODl[:, shoff + c:shoff + c + 1], scale=GSl[:, gi, c:c + 1]),
                    reads=[('ntm', c % 2), ('mod', l, shoff // 32), ('gs', l, gi)], writes=[('hb', c)])

    def phase_inproj(self, l):
        S, ps, HB, CST, PRM = self.S, self.ps, self.HB, self.CST, self.PRM[:, l, :]
        A, B = LA[l], LB[l]
        T1, T2 = TS - A, TS - B
        tl = tiles_from(A)
        wi = self.w_in[l].rearrange("(kt p) c -> p kt c", p=128)
        cv = Carve(self.R2, self.KVOFF)
        Ct = cv.f32(TS)
        St = cv.f32(TS)
        VR = cv.f32(TS)
        E = [cv.f32(TS) for _ in range(5)]
        QO = [self.QOT[:, 0, :], self.QOT[:, 1, :]]
        posi = E[0].bitcast(I32)
        S.op('sp', lambda e: e.dma_start(out=posi[:, 0:T1], in_=self.pos[0:1, A:TS].partition_broadcast(128)),
             writes=['e0'], dsem='posi')
        S.op('sp', lambda e: e.dma_start(out=VR[:, 0:T1], in_=self.valid[0:1, A:TS].partition_broadcast(128)),
             writes=['vr'], dsem='vr')
        S.op('dve', lambda e: e.tensor_copy(out=E[1][:, 0:T1], in_=posi[:, 0:T1]), reads=['e0'], writes=['e1'])
        S.op('dve', lambda e: e.tensor_scalar(out=E[2][:, 0:T1], in0=E[1][:, 0:T1], scalar1=CST[:, 810:811], scalar2=None, op0=ALU.mult),
             reads=['e1', 'cst'], writes=['e2'])
        for (dst, dkey, add) in ((St, 'st', 0.0), (Ct, 'ct', math.pi / 2)):
            S.op('dve', lambda e, add=add: e.tensor_scalar(out=E[3][:, 0:T1], in0=E[2][:, 0:T1], scalar1=1.0 / (2 * math.pi),
                                                           scalar2=add / (2 * math.pi), op0=ALU.mult, op1=ALU.add),
                 reads=['e2'], writes=['e3'])
            S.op('dve', lambda e: e.tensor_copy(out=posi[:, 0:T1], in_=E[3][:, 0:T1]), reads=['e3'], writes=['e0'])
            S.op('dve', lambda e: e.tensor_copy(out=E[4][:, 0:T1], in_=posi[:, 0:T1]), reads=['e0'], writes=['e4'])
            S.op('dve', lambda e: e.scalar_tensor_tensor(out=E[3][:, 0:T1], in0=E[4][:, 0:T1], scalar=-C1, in1=E[2][:, 0:T1],
                                                         op0=ALU.mult, op1=ALU.add), reads=['e4', 'e2', 'e3'], writes=['e3'])
            S.op('dve', lambda e: e.scalar_tensor_tensor(out=E[3][:, 0:T1], in0=E[4][:, 0:T1], scalar=-C2, in1=E[3][:, 0:T1],
                                                         op0=ALU.mult, op1=ALU.add), reads=['e4', 'e3'], writes=['e3'])
            S.op('dve', lambda e, add=add: e.tensor_scalar(out=E[3][:, 0:T1], in0=E[3][:, 0:T1], scalar1=add, scalar2=PI_LO,
                                                           op0=ALU.add, op1=ALU.min), reads=['e3'], writes=['e3'])
            S.op('dve', lambda e: e.tensor_scalar(out=E[3][:, 0:T1], in0=E[3][:, 0:T1], scalar1=-PI_LO, scalar2=None, op0=ALU.max),
                 reads=['e3'], writes=['e3'])
            S.op('act', lambda e, dst=dst: e.activation(out=dst[:, 0:T1], in_=E[3][:, 0:T1], func=AF.Sin),
                 reads=['e3'], writes=[dkey])
        S.barrier()
        if self.stop == 'inproj0a':
            return

        def rhs_h(kt, ti):
            ts, tn = tl[ti]
            return HB[:, kt, ts - A:ts - A + tn]

        def rk_h(kt, ti):
            return [('hb', kt)]

        slots = {}

        def qk_main(ci):
            slots[ci] = self.acquire(wi[:, :, ci * 128:(ci + 1) * 128], 32)
            self.main_mm(slots[ci], 32, ci % 2, tl, rhs_h, rk_h)

        import os as _os2
        QKE = int(_os2.environ.get('QKE', '99'))

        def qk_epi(ci):
            pset = ci % 2
            stepc = [0]
            realop = S.op

            def gop(*a, **k):
                stepc[0] += 1
                if stepc[0] <= QKE:
                    return realop(*a, **k)
            class _G:
                op = staticmethod(gop)
            S_ = _G
            gcol = 256 if ci < 16 else 257
            isk = ci >= 16
            qo = self.KT[:, ci - 16, :] if isk else QO[ci % 2]
            qokey = ('kt', ci - 16) if isk else ('qo', ci % 2)
            for ti, (ts, tn) in enumerate(tl):
                stepc[0] = 0
                b = pset * 3 + ti
                lo = ts - A
                sl = slice(lo, lo + tn)
                k = f"{ti}"
                S_.op('act', lambda e, b=b, sl=sl, tn=tn: e.activation(out=E[0][:, sl], in_=ps[:, b, 0:tn], func=AF.Square),
                     reads=[('ps', b)], writes=['qe0' + k])
                S_.op('dve', lambda e, b=b, sl=sl, tn=tn: e.tensor_scalar(out=E[1][:, sl], in0=ps[:, b, 0:tn], scalar1=PRM[:, gcol:gcol + 1],
                                                                         scalar2=None, op0=ALU.mult),
                     reads=[('ps', b), 'prm', 'qe0' + k], writes=['qe1' + k])
                ab = 6 + (ti % 2)
                S_.op('pe', lambda e, ab=ab, sl=sl, tn=tn: e.matmul(ps[:, ab, 0:tn], lhsT=CST[:, 128:256], rhs=E[0][:, sl], start=True, stop=True),
                     reads=['qe0' + k, 'cst'], writes=[('ps', ab)])
                S_.op('act', lambda e, ab=ab, sl=sl, tn=tn: e.activation(out=E[2][:, sl], in_=ps[:, ab, 0:tn], func=AF.Sqrt,
                                                                        bias=CST[:, 811:812], scale=1.0 / 64),
                     reads=[('ps', ab), 'cst'], writes=['qe2' + k])
                S_.op('dve', lambda e, sl=sl: e.reciprocal(out=E[2][:, sl], in_=E[2][:, sl]), reads=['qe2' + k], writes=['qe2' + k])
                S_.op('dve', lambda e, sl=sl: e.tensor_tensor(out=E[1][:, sl], in0=E[1][:, sl], in1=E[2][:, sl], op=ALU.mult),
                     reads=['qe1' + k, 'qe2' + k], writes=['qe1' + k])
                ab2 = 6 + ((ti + 1) % 2)
                S_.op('pe', lambda e, ab2=ab2, sl=sl, tn=tn: e.matmul(ps[:, ab2, 0:tn], lhsT=CST[:, 256:384], rhs=E[1][:, sl], start=True, stop=True),
                     reads=['qe1' + k, 'cst'], writes=[('ps', ab2)])
                S_.op('dve', lambda e, sl=sl: e.tensor_tensor(out=E[0][:, sl], in0=E[1][:, sl], in1=Ct[:, sl], op=ALU.mult),
                     reads=['qe1' + k, 'ct', 'qe0' + k], writes=['qe0' + k])
                S_.op('dve', lambda e, ab2=ab2, sl=sl, tn=tn: e.tensor_tensor(out=E[2][:, sl], in0=ps[:, ab2, 0:tn], in1=St[:, sl], op=ALU.mult),
                     reads=[('ps', ab2), 'st', 'qe2' + k], writes=['qe2' + k])
                S_.op('dve', lambda e, sl=sl, qo=qo: e.tensor_tensor(out=qo[:, sl], in0=E[0][:, sl], in1=E[2][:, sl], op=ALU.add),
                     reads=['qe0' + k, 'qe2' + k], writes=[qokey])
                if isk:
                    ab3 = 6 + (ti % 2)
                    S_.op('pe', lambda e, ab3=ab3, sl=sl, tn=tn, qo=qo: e.matmul(ps[:, ab3, 0:tn], lhsT=self.PSWB[:, :], rhs=qo[:, sl], start=True, stop=True),
                         reads=[qokey, 'pswb'], writes=[('ps', ab3)])
                    S_.op('act', lambda e, ab3=ab3, sl=sl, tn=tn: e.activation(out=self.KS[:, ci - 16, sl], in_=ps[:, ab3, 0:tn], func=AF.Copy),
                         reads=[('ps', ab3)], writes=[('ks', ci - 16)])
            if not isk:
                S_.op('sp', lambda e, qo=qo: e.dma_start(out=self.qscr[ci][:, 0:T1], in_=qo[:, 0:T1]),
                     reads=[qokey], writes=[('qscr', ci)], dsem=f"qo{ci % 2}")

        import os as _os
        QKN = int(_os.environ.get('QKN', '18'))
        qk_main(0)
        for ci in range(QKN):
            if ci + 1 < QKN:
                qk_main(ci + 1)
            qk_epi(ci)
        if QKN < 18:
            return

        if self.stop == 'inproj0b':
            return
        nb = T1 // 128
        for vu in range(2):
            slot = self.acquire(wi[:, :, (18 + vu) * 128:(19 + vu) * 128], 32)
            pset = vu % 2
            for tb in range(nb):
                b = pset * 3 + tb // 4
                for kt in range(32):
                    S.op('pe', lambda e, slot=slot, tb=tb, kt=kt, b=b: e.matmul(
                        ps[:, b, (tb % 4) * 128:(tb % 4 + 1) * 128], lhsT=HB[:, kt, tb * 128:(tb + 1) * 128],
                        rhs=self.WR[:, slot, kt, :], start=(kt == 0), stop=(kt == 31)),
                        reads=[('w', slot, kt // 16), ('hb', kt)], writes=[('ps', b)], signal=(kt == 31))
            for bi in range((nb + 3) // 4):
                b = pset * 3 + bi
                n = min(4, nb - bi * 4)
                blk0 = A // 128 + bi * 4
                S.op('act', lambda e, b=b, n=n, blk0=blk0, vu=vu: e.activation(
                    out=self.VT[:, blk0:blk0 + n, vu * 128:(vu + 1) * 128],
                    in_=ps[:, b, 0:n * 128].rearrange("p (j t) -> p j t", j=n), func=AF.Copy),
                    reads=[('ps', b)], writes=[('vt', vu)])
        S.barrier()
        if self.stop == 'inproj0c':
            return

        cv2 = Carve(self.R2, self.KVOFF)
        cv2.off = 3 * TS
        ACP = cv2.f32(TS)
        SGM = cv2.f32(TS)
        HG = cv2.f32(TS + 32)
        ACC = [cv2.f32(1152) for _ in range(2)]
        S.op('dve', lambda e: e.memset(HG[:, 0:32], 0.0), writes=['hg'])
        off = B - A - 30 + 30
        for i in range(16):
            sa = self.acquire(wi[:, :, (20 + i) * 128:(21 + i) * 128], 32)
            self.main_mm(sa, 32, 0, tl, rhs_h, rk_h)
            sg_ = self.acquire(wi[:, :, (36 + i) * 128:(37 + i) * 128], 32)
            self.main_mm(sg_, 32, 1, tl, rhs_h, rk_h)
            if l == 0:
                self.ada_tick(2, (0, 96))
            for ti, (ts, tn) in enumerate(tl):
                sl = slice(ts - A, ts - A + tn)
                S.op('act', lambda e, ti=ti, sl=sl, tn=tn: e.activation(out=ACP[:, sl], in_=ps[:, ti, 0:tn], func=AF.Copy),
                     reads=[('ps', ti)], writes=['acp'])
            for ti, (ts, tn) in enumerate(tl):
                sl = slice(ts - A, ts - A + tn)
                S.op('act', lambda e, ti=ti, sl=sl, tn=tn: e.activation(out=SGM[:, sl], in_=ps[:, 3 + ti, 0:tn], func=AF.Sigmoid),
                     reads=[('ps', 3 + ti)], writes=['sgm'])
            S.op('dve', lambda e: e.tensor_tensor(out=SGM[:, 0:T1], in0=SGM[:, 0:T1], in1=VR[:, 0:T1], op=ALU.mult),
                 reads=['sgm', 'vr'], writes=['sgm'])
            S.op('dve', lambda e: e.tensor_tensor(out=HG[:, 30:30 + T1], in0=ACP[:, 0:T1], in1=SGM[:, 0:T1], op=ALU.mult),
                 reads=['sgm', 'acp', 'hg'], writes=['hg'])
            acc = ACC[i % 2]
            wb = 354 + i * 31
            S.op('dve', lambda e, acc=acc, wb=wb, i=i: e.tensor_scalar(
                out=acc[:, 0:T2], in0=HG[:, off:off + T2], scalar1=PRM[:, wb:wb + 1], scalar2=PRM[:, 274 + i:275 + i],
                op0=ALU.mult, op1=ALU.add), reads=['hg', 'prm'], writes=[('acc', i % 2)])
            for j in range(1, 31):
                S.op('dve', lambda e, acc=acc, wb=wb, j=j: e.scalar_tensor_tensor(
                    out=acc[:, 0:T2], in0=HG[:, off + j:off + j + T2], scalar=PRM[:, wb + j:wb + j + 1], in1=acc[:, 0:T2],
                    op0=ALU.mult, op1=ALU.add), reads=['hg', 'prm', ('acc', i % 2)], writes=[('acc', i % 2)])
            S.op('sp', lambda e, acc=acc, i=i: e.dma_start(out=self.mscr[(16 + i) * 128:(17 + i) * 128, 0:T2], in_=acc[:, 0:T2]),
                 reads=[('acc', i % 2)], writes=[('ms', 16 + i)], dsem=f"acc{i % 2}")

    def phase_attn(self, l):
        S, ps, CST, PRM = self.S, self.ps, self.CST, self.PRM[:, l, :]
        A, B = LA[l], LB[l]
        T1, T2 = TS - A, TS - B
        hbf = self.HB[:, :, :].rearrange("p c t -> p (c t)")
        o = 0
        VP = hbf[:, o:o + 10240].rearrange("p (b g e c) -> p b g e c", b=10, g=4, e=2)
        o += 10240
        QG = []
        for _ in range(2):
            QG.append(hbf[:, o:o + 4 * TS].rearrange("p (c t) -> p c t", c=4))
            o += 4 * TS
        EB = []
        PT = []
        for _ in range(2):
            EB.append(hbf[:, o:o + 2048].rearrange("p (b t) -> p b t", b=4))
            o += 2048
        for _ in range(2):
            PT.append(hbf[:, o:o + 2048].rearrange("p (b t) -> p b t", b=4))
            o += 2048
        MK = hbf[:, o:o + 1024].rearrange("p (m t) -> p m t", m=2)
        o += 1024
        OP = hbf[:, o:o + 256].rearrange("p (e c) -> p e c", e=2)
        o += 256
        cv = Carve(self.R2, self.KVOFF)
        DEN = [cv.f32(512) for _ in range(2)]
        AO = [cv.f32(512) for _ in range(2)]
        ESK = cv.f32(16)
        S.op('dve', lambda e: e.memset(VP.rearrange("p b g e c -> p (b g e c)"), 0.0), writes=['vp'])
        S.op('dve', lambda e: e.memset(OP.rearrange("p e c -> p (e c)"), 0.0), writes=['op'])
        for e_ in range(2):
            S.op('dve', lambda e, e_=e_: e.memset(OP[:, e_, e_ * 64:(e_ + 1) * 64], 1.0), reads=['op'], writes=['op'])
            for g in range(4):
                S.op('dve', lambda e, e_=e_, g=g: e.tensor_copy(out=VP[:, :, g, e_, e_ * 64:(e_ + 1) * 64], in_=self.VT[:, :, g * 64:(g + 1) * 64]),
                     reads=['vp', ('vt', 0), ('vt', 1)], writes=['vp'])
        for m in range(2):
            for j in range(4):
                S.op('dve', lambda e, m=m, j=j: e.tensor_copy(out=MK[:, m, j * 128:(j + 1) * 128], in_=CST[:, 512 + m * 128:640 + m * 128]),
                     reads=['cst'], writes=['mk'])
        S.op('act', lambda e: e.activation(out=ESK[:, :], in_=PRM[:, 258:274], func=AF.Exp), reads=['prm'], writes=['esk'])
        mview = self.mscr.rearrange("(c p) t -> p c t", p=128)
        it = 0
        for g in range(4):
            qg = QG[g % 2]
            S.op('sp', lambda e, qg=qg, g=g: e.dma_start(out=qg[:, :, 0:T1], in_=self.qscr[4 * g:4 * g + 4, :, 0:T1].rearrange("c p t -> p c t")),
                 reads=[('qscr', 4 * g + j) for j in range(4)], writes=[('qg', g % 2)], dsem=f"qg{g % 2}")
            c = g // 2
            for qb in range(B // 128, 10):
                eb = EB[it % 2]
                pt = PT[it % 2]
                qcol = qb * 128 - A
                bis = []
                for e_ in range(2):
                    ksrc, kkey = (self.KT, ('kt', c)) if e_ == (g % 2) else (self.KS, ('ks', c))
                    for kk, kb in enumerate((qb - 1, qb)):
                        bi = e_ * 2 + kk
                        bis.append((bi, e_, kk, kb))
                        kcol = kb * 128 - A
                        S.op('pe', lambda e, bi=bi, e_=e_, ksrc=ksrc, kcol=kcol, qg=qg, qcol=qcol, c=c: e.matmul(
                            ps[:, bi, :], lhsT=ksrc[e_ * 64:(e_ + 1) * 64, c, kcol:kcol + 128],
                            rhs=qg[e_ * 64:(e_ + 1) * 64, :, qcol:qcol + 128], start=True, stop=True),
                            reads=[kkey, ('qg', g % 2)], writes=[('ps', bi)])
                for (bi, e_, kk, kb) in bis:
                    S.op('act', lambda e, bi=bi, eb=eb: e.activation(out=eb[:, bi, :], in_=ps[:, bi, :], func=AF.Exp, scale=0.125),
                         reads=[('ps', bi)], writes=[('eb', it % 2, bi)])
                    S.op('dve', lambda e, bi=bi, eb=eb, pt=pt, kk=kk, kb=kb: e.scalar_tensor_tensor(
                        out=pt[:, bi, :], in0=eb[:, bi, :], scalar=CST[:, 768 + kb:769 + kb], in1=MK[:, 1 - kk, :],
                        op0=ALU.mult, op1=ALU.mult), reads=[('eb', it % 2, bi), 'cst', 'mk'], writes=[('pt', it % 2, bi)])
                bo = 4 + (it % 2)
                bd = 6 + (it % 2)
                for n_, (bi, e_, kk, kb) in enumerate(bis):
                    S.op('pe', lambda e, bi=bi, e_=e_, kb=kb, pt=pt, n_=n_, bo=bo, g=g: e.matmul(
                        ps[:, bo, :], lhsT=VP[:, kb, g, e_, :], rhs=pt[:, bi, :], start=(n_ == 0), stop=(n_ == 3)),
                        reads=[('pt', it % 2, bi), 'vp'], writes=[('ps', bo)])
                for n_, (bi, e_, kk, kb) in enumerate(bis):
                    S.op('pe', lambda e, bi=bi, e_=e_, pt=pt, n_=n_, bd=bd: e.matmul(
                        ps[:, bd, :], lhsT=OP[:, e_, :], rhs=pt[:, bi, :], start=(n_ == 0), stop=(n_ == 3)),
                        reads=[('pt', it % 2, bi), 'op'], writes=[('ps', bd)])
                den = DEN[it % 2]
                ao = AO[it % 2]
                for j in range(4):
                    S.op('dve', lambda e, j=j, den=den, bd=bd, g=g: e.tensor_scalar(
                        out=den[:, j * 128:(j + 1) * 128], in0=ps[:, bd, j * 128:(j + 1) * 128],
                        scalar1=ESK[:, 4 * g + j:4 * g + j + 1], scalar2=None, op0=ALU.add),
                        reads=[('ps', bd), 'esk'], writes=[('den', it % 2)])
                S.op('dve', lambda e, den=den: e.reciprocal(out=den[:, :], in_=den[:, :]), reads=[('den', it % 2)], writes=[('den', it % 2)])
                S.op('dve', lambda e, den=den, ao=ao, bo=bo: e.tensor_tensor(out=ao[:, :], in0=ps[:, bo, :], in1=den[:, :], op=ALU.mult),
                     reads=[('ps', bo), ('den', it % 2)], writes=[('ao', it % 2)])
                ocol = qb * 128 - B
                S.op('sp', lambda e, ao=ao, ocol=ocol, g=g: e.dma_start(
                    out=mview[:, 4 * g:4 * g + 4, ocol:ocol + 128], in_=ao.rearrange("p (j t) -> p j t", j=4)),
                    reads=[('ao', it % 2)], writes=[('ms', 4 * g + j) for j in range(4)], dsem=f"ao{it % 2}")
                it += 1

    def phase_pnorm(self, l):
        S, ps, HB, CST, PRM = self.S, self.ps, self.HB, self.CST, self.PRM[:, l, :]
        B = LB[l]
        cv = Carve(self.R2, self.KVOFF)
        CB = cv.f32(16 * 512).rearrange("p (c t) -> p c t", c=16)
        SQ = [cv.f32(512) for _ in range(2)]
        RT = cv.f32(512)
        RS = cv.f32(512)
        MU = cv.f32(512)
        NM = cv.f32(512)
        mview = self.mscr.rearrange("(c p) t -> p c t", p=128)
        cbk = [('cb', c) for c in range(16)]

        def rstd_from(bank, tn, eps_col, n):
            S.op('act', lambda e: e.activation(out=RT[:, 0:tn], in_=ps[:, bank, 0:tn], func=AF.Sqrt,
                                               bias=CST[:, eps_col:eps_col + 1], scale=1.0 / n),
                 reads=[('ps', bank), 'cst'], writes=['prt'])
            S.op('dve', lambda e: e.reciprocal(out=RS[:, 0:tn], in_=RT[:, 0:tn]), reads=['prt'], writes=['prs'])

        def sq_acc(c, tn, bank):
            sq = SQ[c % 2]
            S.op('act', lambda e, sq=sq, c=c, tn=tn: e.activation(out=sq[:, 0:tn], in_=CB[:, c, 0:tn], func=AF.Square),
                 reads=[('cb', c)], writes=[('psq', c % 2)])
            S.op('pe', lambda e, sq=sq, c=c, tn=tn, bank=bank: e.matmul(ps[:, bank, 0:tn], lhsT=self.ONES[:, :], rhs=sq[:, 0:tn],
                                                                        start=(c == 0), stop=(c == 15)),
                 reads=[('psq', c % 2), 'ones'], writes=[('ps', bank)])

        for (ts, tn) in tiles_from(B):
            o0 = ts - B
            S.op('sp', lambda e, tn=tn, o0=o0: e.dma_start(out=CB[:, :, 0:tn], in_=mview[:, 0:16, o0:o0 + tn]),
                 reads=[('ms', c) for c in range(16)], writes=cbk, dsem="pcb")
            for c in range(16):
                sq_acc(c, tn, 6)
            rstd_from(6, tn, 811, 2048)
            for c in range(16):
                S.op('dve', lambda e, c=c, tn=tn, o0=o0: e.scalar_tensor_tensor(
                    out=HB[:, c, o0:o0 + tn], in0=CB[:, c, 0:tn], scalar=PRM[:, 322 + c:323 + c], in1=RS[:, 0:tn],
                    op0=ALU.mult, op1=ALU.mult), reads=[('cb', c), 'prs', 'prm'], writes=[('hb', c)])
            S.op('sp', lambda e, tn=tn, o0=o0: e.dma_start(out=CB[:, :, 0:tn], in_=mview[:, 16:32, o0:o0 + tn]),
                 reads=[('ms', 16 + c) for c in range(16)], writes=cbk, dsem="pcb")
            for c in range(16):
                S.op('pe', lambda e, c=c, tn=tn: e.matmul(ps[:, 7, 0:tn], lhsT=self.ONES[:, :], rhs=CB[:, c, 0:tn], start=(c == 0), stop=(c == 15)),
                     reads=[('cb', c), 'ones'], writes=[('ps', 7)])
                sq_acc(c, tn, 6)
            S.op('dve', lambda e, tn=tn: e.tensor_scalar(out=MU[:, 0:tn], in0=ps[:, 7, 0:tn], scalar1=1.0 / 2048, scalar2=None, op0=ALU.mult),
                 reads=[('ps', 7)], writes=['pmu'])
            S.op('dve', lambda e, tn=tn: e.tensor_tensor(out=NM[:, 0:tn], in0=MU[:, 0:tn], in1=MU[:, 0:tn], op=ALU.mult),
                 reads=['pmu'], writes=['pnm'])
            S.op('dve', lambda e, tn=tn: e.scalar_tensor_tensor(out=NM[:, 0:tn], in0=ps[:, 6, 0:tn], scalar=1.0 / 2048, in1=NM[:, 0:tn],
                                                                op0=ALU.mult, op1=ALU.subtract), reads=[('ps', 6), 'pnm'], writes=['pnm'])
            S.op('act', lambda e, tn=tn: e.activation(out=RT[:, 0:tn], in_=NM[:, 0:tn], func=AF.Sqrt, bias=CST[:, 812:813], scale=1.0),
                 reads=['pnm', 'cst'], writes=['prt'])
            S.op('dve', lambda e, tn=tn: e.reciprocal(out=RS[:, 0:tn], in_=RT[:, 0:tn]), reads=['prt'], writes=['prs'])
            S.op('dve', lambda e, tn=tn: e.scalar_tensor_tensor(out=NM[:, 0:tn], in0=MU[:, 0:tn], scalar=-1.0, in1=RS[:, 0:tn],
                                                                op0=ALU.mult, op1=ALU.mult), reads=['pmu', 'prs', 'pnm'], writes=['pnm'])
            for c in range(16):
                S.op('dve', lambda e, c=c, tn=tn: e.tensor_tensor(out=CB[:, c, 0:tn], in0=CB[:, c, 0:tn], in1=RS[:, 0:tn], op=ALU.mult),
                     reads=[('cb', c), 'prs'], writes=[('cb', c)])
                S.op('dve', lambda e, c=c, tn=tn: e.tensor_tensor(out=CB[:, c, 0:tn], in0=CB[:, c, 0:tn], in1=NM[:, 0:tn], op=ALU.add),
                     reads=[('cb', c), 'pnm'], writes=[('cb', c)])
                S.op('act', lambda e, c=c, tn=tn: e.activation(out=CB[:, c, 0:tn], in_=CB[:, c, 0:tn], func=AF.Silu,
                                                               bias=PRM[:, 306 + c:307 + c], scale=PRM[:, 290 + c:291 + c]),
                     reads=[('cb', c), 'prm'], writes=[('cb', c)])
                sq_acc(c, tn, 6)
            rstd_from(6, tn, 811, 2048)
            for c in range(16):
                S.op('dve', lambda e, c=c, tn=tn, o0=o0: e.scalar_tensor_tensor(
                    out=HB[:, 16 + c, o0:o0 + tn], in0=CB[:, c, 0:tn], scalar=PRM[:, 338 + c:339 + c], in1=RS[:, 0:tn],
                    op0=ALU.mult, op1=ALU.mult), reads=[('cb', c), 'prs', 'prm'], writes=[('hb', 16 + c)])

    def resid_epi(self, l, pset, tl, t0, gate_off, oc, obi):
        S, ps = self.S, self.ps
        obslot = obi % self.NOB
        for ti, (ts, tn) in enumerate(tl):
            b = pset * 3 + ti
            o_ = self.OB[:, obslot, ts - t0:ts - t0 + tn]
            if ti % 2 == 0:
                S.op('act', lambda e, b=b, o_=o_, tn=tn: e.activation(out=o_, in_=ps[:, b, 0:tn], func=AF.Identity,
                                                                      scale=self.MODS[:, l, gate_off + oc:gate_off + oc + 1]),
                     reads=[('ps', b), ('mod', l, gate_off // 32), ('ob', obslot)], writes=[('ob', obslot)])
            else:
                S.op('dve', lambda e, b=b, o_=o_, tn=tn: e.tensor_scalar(out=o_, in0=ps[:, b, 0:tn],
                                                                         scalar1=self.MODS[:, l, gate_off + oc:gate_off + oc + 1],
                                                                         scalar2=None, op0=ALU.mult),
                     reads=[('ps', b), ('mod', l, gate_off // 32), ('ob', obslot)], writes=[('ob', obslot)])
        self.x_accum(obslot, oc, t0)

    def phase_outproj(self, l):
        HB = self.HB
        B = LB[l]
        tl = tiles_from(B)

        def rhs(kt, ti):
            ts, tn = tl[ti]
            return HB[:, kt, ts - B:ts - B + tn]
        wo = self.w_out[l].rearrange("(kt p) c -> p kt c", p=128)
        self.ada_until((l, 96))
        for oc in range(32):
            slot = self.acquire(wo[:, :, oc * 128:(oc + 1) * 128], 32)
            self.main_mm(slot, 32, oc % 2, tl, rhs, lambda kt, ti: [('hb', kt)])
            if l == 0:
                self.ada_tick(2, (0, 160))
            self.resid_epi(l, oc % 2, tl, B, 64, oc, self.obi)
            self.obi += 1
        self.flush_stores()

    def phase_ffn(self, l):
        S, ps, HB = self.S, self.ps, self.HB
        B = LB[l]
        T2 = TS - B
        tl = tiles_from(B)
        cv = Carve(self.R2, 15360)
        ACTT = cv.bf16(16 * 1152).rearrange("p (j t) -> p j t", j=16)
        SG = [cv.f32(1152) for _ in range(2)]

        def rhs_h(kt, ti):
            ts, tn = tl[ti]
            return HB[:, kt, ts - B:ts - B + tn]
        pi = 0
        wg = self.w_g[l].rearrange("(kt p) c -> p kt c", p=128)
        wu = self.w_u[l].rearrange("(kt p) c -> p kt c", p=128)
        wd = self.w_d[l].rearrange("(kt p) c -> p kt c", p=128)

        def tick():
            if l == 0:
                if self.ada_q and self.ada_q[0] < (0, 192):
                    self.ada_tick(2, (0, 192))
                elif self.depth > 1:
                    self.ada_tick(1, (1, 192))
        for (f0, nf) in self.SC:
            for j in range(nf):
                f = f0 + j
                sg_slot = self.acquire(wg[:, :, f * 128:(f + 1) * 128], 32)
                self.main_mm(sg_slot, 32, 0, tl, rhs_h, lambda kt, ti: [('hb', kt)])
                tick()
                su_slot = self.acquire(wu[:, :, f * 128:(f + 1) * 128], 32)
                self.main_mm(su_slot, 32, 1, tl, rhs_h, lambda kt, ti: [('hb', kt)])
                tick()
                sg = SG[pi % 2]
                for ti, (ts, tn) in enumerate(tl):
                    sl = slice(ts - B, ts - B + tn)
                    S.op('act', lambda e, ti=ti, sl=sl, tn=tn, sg=sg: e.activation(out=sg[:, sl], in_=ps[:, ti, 0:tn], func=AF.Silu),
                         reads=[('ps', ti)], writes=[('sg', pi % 2, ti)])
                    S.op('dve', lambda e, ti=ti, sl=sl, tn=tn, sg=sg, j=j: e.tensor_tensor(out=ACTT[:, j, sl], in0=sg[:, sl], in1=ps[:, 3 + ti, 0:tn], op=ALU.mult),
                         reads=[('ps', 3 + ti), ('sg', pi % 2, ti)], writes=[('actt', j)])
                pi += 1

            def rhs_a(kt, ti):
                ts, tn = tl[ti]
                return ACTT[:, kt, ts - B:ts - B + tn]
            self.ada_until((l, 192))
            for oc in range(32):
                slot = self.acquire(wd[:, f0:f0 + nf, oc * 128:(oc + 1) * 128], nf)
                self.main_mm(slot, nf, oc % 2, tl, rhs_a, lambda kt, ti: [('actt', kt)])
                tick()
                self.resid_epi(l, oc % 2, tl, B, 160, oc, self.obi)
                self.obi += 1
        self.flush_stores()

    def phase_final(self):
        S, ps, CST = self.S, self.ps, self.CST
        cv = Carve(self.R2, 15360)
        XS = [cv.f32(4096).rearrange("p (c t) -> p c t", c=32) for _ in range(2)]
        XO = cv.f32(4096)
        xTv = self.xT.rearrange("(c p) t -> p c t", p=128)
        for tb in range(8):
            xs = XS[tb % 2]
            s0 = 256 + tb * 128
            S.op('sp', lambda e, xs=xs, s0=s0: e.dma_start(out=xs[:, :, :], in_=xTv[:, :, s0:s0 + 128]),
                 reads=[('x', c) for c in range(32)], writes=[('fxs', tb % 2)], dsem=f"fxs{tb % 2}")
            for c in range(32):
                S.op('pe', lambda e, xs=xs, c=c: e.transpose(
                    ps[:, c // 4, (c % 4) * 128:(c % 4 + 1) * 128], xs[:, c, :], CST[:, 0:128]),
                    reads=[('fxs', tb % 2), 'cst'], writes=[('ps', c // 4)], signal=(c % 4 == 3))
            for b in range(8):
                o_ = XO[:, b * 512:(b + 1) * 512]
                if b % 2 == 0:
                    S.op('act', lambda e, o_=o_, b=b: e.activation(out=o_, in_=ps[:, b, :], func=AF.Copy),
                         reads=[('ps', b), 'fxo'], writes=[('fxo', b)])
                else:
                    S.op('dve', lambda e, o_=o_, b=b: e.tensor_copy(out=o_, in_=ps[:, b, :]),
                         reads=[('ps', b), 'fxo'], writes=[('fxo', b)])
            S.op('sp', lambda e, tb=tb: e.dma_start(out=self.out[tb * 128:(tb + 1) * 128, :], in_=XO),
                 reads=[('fxo', b) for b in range(8)], writes=['fxo'], dsem="fout")
        k = S.dma_sem("fout")
        S.wait_all('sp', [(k, S.dcnt[k])])


def _consts():
    c = np.zeros((128, NCST), np.float32)
    c[:, 0:128] = np.eye(128, dtype=np.float32)
    p = np.arange(128)
    c[:, 128:256] = (p[:, None] // 64 == p[None, :] // 64).astype(np.float32)
    prot = np.zeros((128, 128), np.float32)
    for m in range(128):
        d = m % 64
        if d < 8:
            prot[m + 8, m] = -1.0
        elif d < 16:
            prot[m - 8, m] = 1.0
    c[:, 256:384] = prot
    psw = np.zeros((128, 128), np.float32)
    for m in range(128):
        psw[(m + 64) % 128, m] = 1.0
    c[:, 384:512] = psw
    c[:, 512:640] = (p[:, None] <= p[None, :]).astype(np.float32)
    c[:, 640:768] = (p[:, None] > p[None, :]).astype(np.float32)
    invf = (np.float32(500000.0) ** (-(np.arange(0, 16, 2, dtype=np.float32)) / np.float32(16))).astype(np.float32)
    for q in range(128):
        d = q % 64
        c[q, 810] = invf[d % 8] if d < 16 else 0.0
    c[:, 811] = 1e-6
    c[:, 812] = 1e-5
    return c


def _params(inp):
    prm = np.zeros((2, 128, NPRM), np.float32)

    def cl(v, n):
        return np.ascontiguousarray(np.asarray(v, np.float32).reshape(n, 128).T)
    for l in range(2):
        prm[l, :, 0:192] = cl(inp['b_ada'][l], 192)
        prm[l, :, 192:224] = cl(inp['norm1_g'][l], 32)
        prm[l, :, 224:256] = cl(inp['norm2_g'][l], 32)
        prm[l, :, 256] = np.tile(np.asarray(inp['q_norm_g'][l], np.float32), 2)
        prm[l, :, 257] = np.tile(np.asarray(inp['k_norm_g'][l], np.float32), 2)
        sk = np.asarray(inp['sinks'][l], np.float32).reshape(16, 2)
        prm[l, 0:64, 258:274] = sk[None, :, 0]
        prm[l, 64:128, 258:274] = sk[None, :, 1]
        prm[l, :, 274:290] = cl(inp['conv_b'][l], 16)
        prm[l, :, 290:306] = cl(inp['conv_ln_g'][l], 16)
        prm[l, :, 306:322] = cl(inp['conv_ln_b'][l], 16)
        prm[l, :, 322:338] = cl(inp['attn_out_g'][l], 16)
        prm[l, :, 338:354] = cl(inp['conv_out_g'][l], 16)
        cw = np.asarray(inp['conv_w'][l], np.float32).reshape(31, 16, 128)
        prm[l, :, 354:850] = np.ascontiguousarray(cw.transpose(2, 1, 0)).reshape(128, 496)
    return prm


_NC_CACHE = {}


def kernel(**inputs):
    inp = {k: np.asarray(v) for k, v in inputs.items()}
    x = inp['x'].astype(np.float32, copy=False)
    pos = inp['positions'].astype(np.int32, copy=False)
    if 'nc' not in _NC_CACHE:
        b = Builder()
        b.obi = 0
        _NC_CACHE['nc'] = b.build()
    nc = _NC_CACHE['nc']
    cbase = _consts()
    prm = _params(inp)
    shared = {k: np.ascontiguousarray(inp[k], dtype=np.float32) for k in
              ('w_ada', 'w_in', 'w_out', 'w_ffn_gate', 'w_ffn_up', 'w_ffn_down')}
    in_maps = []
    for r in range(8):
        b_, j = r // 4, r % 4
        s = 1024 * j
        xin = np.zeros((TS, D), np.float32)
        pp = np.zeros((1, TS), np.int32)
        vv = np.zeros((1, TS), np.float32)
        lo = max(0, s - 256)
        n = s + 1024 - lo
        xin[TS - n:] = x[b_, lo:s + 1024]
        pp[0, TS - n:] = pos[b_, lo:s + 1024]
        vv[0, TS - n:] = 1.0
        cst = cbase.copy()
        cst[:, 768:778] = vv.reshape(10, 128).T
        cst[:, 778:810] = inp['c'][b_].astype(np.float32).reshape(32, 128).T
        m = {'xin': xin, 'pos': pp, 'valid': vv, 'cst': cst, 'prm': prm}
        m.update(shared)
        in_maps.append(m)
    res = run_bass_kernel_spmd(nc, in_maps, core_ids=list(range(8)))
    out = np.empty((2, 4096, D), np.float32)
    for r in range(8):
        b_, j = r // 4, r % 4
        out[b_, 1024 * j:1024 * (j + 1)] = res.results[r]['out']
    return out
```

```python
import math
import numpy as np
import concourse.bass as bass
import concourse.mybir as mybir
from concourse.bass_utils import run_bass_kernel_spmd

F32 = mybir.dt.float32
BF16 = mybir.dt.bfloat16
I32 = mybir.dt.int32
AF = mybir.ActivationFunctionType
ALU = mybir.AluOpType

D = 4096
NCH = 32
TS = 1280
DFF = 11008
NFF = 86
INC = 6656
LA = [0, 128]
LB = [128, 256]
NCST = 816
NPRM = 850
PI_LO = 3.1415925
C1 = 6.28125
C2 = 2.0 * math.pi - 6.28125
ENGS = ['pe', 'act', 'dve', 'pool', 'sp']


def tiles_from(t0):
    n = TS - t0
    if n == 1280:
        sz = [512, 512, 256]
    elif n == 1152:
        sz = [384, 384, 384]
    else:
        sz = [512, 512]
    out = []
    s = t0
    for z in sz:
        out.append((s, z))
        s += z
    return out


class Sched:
    def __init__(self, nc):
        self.nc = nc
        self.prog = {e: [] for e in ENGS}
        self.semh = {}
        for e in ENGS:
            self.semh[e] = nc.alloc_semaphore(name=f"s_{e}")
        self.cnt = {e: 0 for e in ENGS}
        self.dcnt = {}
        self.seen = {e: {} for e in ENGS}
        self.res = {}

    def dma_sem(self, name):
        k = ('d', name)
        if k not in self.semh:
            self.semh[k] = self.nc.alloc_semaphore(name=f"d_{name}")
            self.dcnt[k] = 0
        return k

    def op(self, e, fn, reads=(), writes=(), signal=True, dsem=None):
        deps = {}

        def need(kv):
            if kv is None:
                return
            k, v = kv
            if deps.get(k, 0) < v:
                deps[k] = v
        for r in reads:
            st = self.res.get(r)
            if st:
                need(st['w'])
        for w in writes:
            st = self.res.get(w)
            if st:
                need(st['w'])
                for k, v in st['r'].items():
                    need((k, v))
        for k, v in deps.items():
            if e == 'pe' and k == 'pe':
                continue
            if self.seen[e].get(k, 0) < v:
                self.prog[e].append(('wait', k, v))
                self.seen[e][k] = v
        if dsem is not None:
            k = self.dma_sem(dsem)
            self.dcnt[k] += 16
            tag = (k, self.dcnt[k])
            self.prog[e].append(('op', fn, k, 16))
        elif signal:
            self.cnt[e] += 1
            tag = (e, self.cnt[e])
            self.prog[e].append(('op', fn, e, 1))
        else:
            tag = (e, self.cnt[e] + 1)
            self.prog[e].append(('op', fn, None, 0))
        for r in reads:
            st = self.res.setdefault(r, {'w': None, 'r': {}})
            if st['r'].get(tag[0], 0) < tag[1]:
                st['r'][tag[0]] = tag[1]
        for w in writes:
            self.res[w] = {'w': tag, 'r': {}}
        return tag

    def wait_all(self, e, tags):
        for k, v in tags:
            if v > 0 and self.seen[e].get(k, 0) < v:
                self.prog[e].append(('wait', k, v))
                self.seen[e][k] = v

    def barrier(self, engines=('pe', 'act', 'dve', 'sp')):
        tags = [(e, self.cnt[e]) for e in engines]
        for k, v in self.dcnt.items():
            if k[1].startswith('w') or k[1].startswith('ob'):
                continue
            tags.append((k, v))
        for e in engines:
            self.wait_all(e, [t for t in tags if t[0] != e or e == 'sp'])

    def emit(self):
        nc = self.nc
        with nc.Block() as block:
            def run(e, engine):
                for item in self.prog[e]:
                    if item[0] == 'wait':
                        engine.wait_ge(self.semh[item[1]], item[2])
                    else:
                        ins = item[1](engine)
                        if item[2] is not None:
                            ins.then_inc(self.semh[item[2]], item[3])

            @block.tensor
            def _(t):
                run('pe', t)

            @block.scalar
            def _(t):
                run('act', t)

            @block.vector
            def _(t):
                run('dve', t)

            @block.gpsimd
            def _(t):
                run('pool', t)

            @block.sync
            def _(t):
                run('sp', t)


class DummySched:
    def __init__(self):
        self.dcnt = {}
        self.prog = {e: [] for e in ENGS}
        self.semh = {}

    def op(self, *a, **k):
        return None

    def barrier(self, *a, **k):
        pass

    def wait_all(self, *a, **k):
        pass

    def dma_sem(self, name):
        self.dcnt[('d', name)] = 0
        return ('d', name)


class Carve:
    def __init__(self, base, limit):
        self.base = base
        self.off = 0
        self.limit = limit

    def f32(self, n):
        ap = self.base[:, self.off:self.off + n]
        self.off += n
        assert self.off <= self.limit, (self.off, self.limit)
        return ap

    def i32(self, n):
        return self.f32(n).bitcast(I32)

    def bf16(self, n):
        w = (n + 1) // 2
        ap = self.base[:, self.off:self.off + w].bitcast(BF16)
        self.off += w
        assert self.off <= self.limit, (self.off, self.limit)
        return ap


class Builder:
    def __init__(self, depth=2, debug=False, stop=None):
        self.depth = depth
        self.debug = debug
        self.stop = stop
        nc = bass.Bass("TRN2", target_bir_lowering=False)
        self.nc = nc

        def din(name, shape, dt):
            return nc.dram_tensor(name, shape, dt, kind="ExternalInput").ap()
        self.xin = din("xin", [TS, D], F32)
        self.pos = din("pos", [1, TS], I32)
        self.valid = din("valid", [1, TS], F32)
        self.cst = din("cst", [128, NCST], F32)
        self.prm = din("prm", [2, 128, NPRM], F32)
        order = ['x0', 'adaln', 'norm1_', 'inproj', 'attn', 'pnorm', 'outproj', 'norm2_', 'ffn', 'final']
        lvl = 99
        if debug and stop is not None and depth == 1:
            lvl = [i for i, o in enumerate(order) if stop.startswith(o)][0]
        self.lvl = lvl
        nl = depth if debug else 2
        self.nl = nl

        def wshape(need, shp):
            return [nl] + (shp if lvl >= need else [128, 128])
        self.w_ada = din("w_ada", wshape(1, [D, 6 * D]), F32)
        self.w_in = din("w_in", wshape(3, [D, INC]), F32)
        self.w_out = din("w_out", wshape(6, [D, D]), F32)
        self.w_g = din("w_ffn_gate", wshape(8, [D, DFF]), F32)
        self.w_u = din("w_ffn_up", wshape(8, [D, DFF]), F32)
        self.w_d = din("w_ffn_down", wshape(8, [DFF, D]), F32)
        self.out = nc.dram_tensor("out", [1024, D], F32, kind="ExternalOutput").ap()
        kw = {"kind": "ExternalOutput"} if debug else {}
        self.xT = nc.dram_tensor("xT", [D, TS], F32, **kw).ap()
        self.qscr = nc.dram_tensor("qscr", [16, 128, TS], BF16, **kw).ap()
        self.mscr = nc.dram_tensor("mscr", [D, 1152], F32, **kw).ap()
        if debug:
            self.dhb = nc.dram_tensor("dhb", [128, NCH, TS], BF16, **kw).ap()
            self.dr2 = nc.dram_tensor("dr2", [128, 15360], F32, **kw).ap()
            self.dmod = nc.dram_tensor("dmod", [128, 192], F32, **kw).ap()

    def unit_list(self):
        units = []
        for l in range(self.depth):
            if self.lvl < 1:
                break
            wa = self.w_ada[l].rearrange("(kt p) c -> p kt c", p=128)
            for u in range(192):
                units.append((wa[:, :, u * 128:(u + 1) * 128], 32))
            if self.lvl < 3:
                break
            wi = self.w_in[l].rearrange("(kt p) c -> p kt c", p=128)
            order = list(range(20)) + [x for i in range(16) for x in (20 + i, 36 + i)]
            for ch in order:
                units.append((wi[:, :, ch * 128:(ch + 1) * 128], 32))
            if self.lvl < 6:
                break
            wo = self.w_out[l].rearrange("(kt p) c -> p kt c", p=128)
            for oc in range(32):
                units.append((wo[:, :, oc * 128:(oc + 1) * 128], 32))
            if self.lvl < 8:
                break
            wg = self.w_g[l].rearrange("(kt p) c -> p kt c", p=128)
            wu = self.w_u[l].rearrange("(kt p) c -> p kt c", p=128)
            wd = self.w_d[l].rearrange("(kt p) c -> p kt c", p=128)
            for (f0, nf) in self.SC:
                for j in range(nf):
                    f = f0 + j
                    units.append((wg[:, :, f * 128:(f + 1) * 128], 32))
                    units.append((wu[:, :, f * 128:(f + 1) * 128], 32))
                for oc in range(32):
                    units.append((wd[:, f0:f0 + nf, oc * 128:(oc + 1) * 128], nf))
        return units

    SC = [(0, 16), (16, 16), (32, 16), (48, 16), (64, 16), (80, 6)]
    NSLOT = 4
    NOB = 3

    def acquire(self, src, nkt):
        S = self.S
        if self.dry:
            self.units.append((src, nkt))
            return 0
        self.flush_stores()
        self.wcur += 1
        u = self.wcur
        while self.wnext < len(self.units) and self.wnext <= u + self.NSLOT - 1:
            i = self.wnext
            slot = i % self.NSLOT
            src_, nkt_ = self.units[i]
            if nkt_ == 32:
                for h in range(2):
                    S.op('pool', lambda e, slot=slot, src_=src_, h=h: e.dma_start(
                        out=self.WR[:, slot, h * 16:(h + 1) * 16, :], in_=src_[:, h * 16:(h + 1) * 16, :]),
                        writes=[('w', slot, h)], dsem=f"w{slot}_{h}")
            else:
                S.op('pool', lambda e, slot=slot, src_=src_, nkt_=nkt_: e.dma_start(
                    out=self.WR[:, slot, 0:nkt_, :], in_=src_[:, 0:nkt_, :]),
                    writes=[('w', slot, 0), ('w', slot, 1)], dsem=f"w{slot}_0")
            self.wnext += 1
        return u % self.NSLOT

    def ada_job(self, l, u):
        S, ps = self.S, self.ps
        wa = self.w_ada[l].rearrange("(kt p) c -> p kt c", p=128)
        slot = self.acquire(wa[:, :, u * 128:(u + 1) * 128], 32)
        ci = self.adai % 2
        self.adai += 1
        bk = 6 + ci
        for kt in range(32):
            S.op('pe', lambda e, slot=slot, kt=kt, bk=bk: e.matmul(
                ps[:, bk, 0:1], lhsT=self.WR[:, slot, kt, :], rhs=self.CTB[:, kt:kt + 1],
                start=(kt == 0), stop=(kt == 31)),
                reads=[('w', slot, kt // 16), 'ctb'], writes=[('ps', bk)], signal=(kt == 31))
        S.op('act', lambda e, bk=bk, l=l, u=u: e.activation(
            out=self.MODS[:, l, u:u + 1], in_=ps[:, bk, 0:1], func=AF.Identity, bias=self.PRM[:, l, u:u + 1]),
            reads=[('ps', bk), 'prm'], writes=[('mod', l, u // 32)])

    def ada_tick(self, n, upto):
        while n > 0 and self.ada_q and self.ada_q[0] < upto:
            l, u = self.ada_q.pop(0)
            self.ada_job(l, u)
            n -= 1

    def ada_until(self, upto):
        self.ada_tick(10 ** 9, upto)

    def flush_stores(self):
        for fn in self.pending:
            fn()
        self.pending = []

    def x_accum(self, obslot, oc, t0):
        n = TS - t0

        def fn():
            self.S.op('pool', lambda e: e.dma_start(
                out=self.xT[oc * 128:(oc + 1) * 128, t0:TS], in_=self.OB[:, obslot, 0:n], accum_op=ALU.add),
                reads=[('ob', obslot)], writes=[('x', oc)], dsem=f"ob{obslot}")
        self.pending.append(fn)

    def main_mm(self, slot, nkt, pset, tl, rhs_fn, rkeys_fn):
        S = self.S
        for kt in range(nkt):
            for ti, (ts, tn) in enumerate(tl):
                last = (kt == nkt - 1) and (ti == len(tl) - 1)
                b = pset * 3 + ti
                S.op('pe', lambda e, kt=kt, ti=ti, b=b, tn=tn: e.matmul(
                    self.ps[:, b, 0:tn], lhsT=self.WR[:, slot, kt, :], rhs=rhs_fn(kt, ti),
                    start=(kt == 0), stop=(kt == nkt - 1)),
                    reads=[('w', slot, (kt // 16) if nkt == 32 else 0)] + rkeys_fn(kt, ti),
                    writes=[('ps', b)], signal=last)

    def build(self):
        nc = self.nc
        with (
            nc.sbuf_tensor("HB", [128, NCH, TS], BF16) as HB,
            nc.sbuf_tensor("R2", [128, 15360], F32) as R2,
            nc.sbuf_tensor("WR", [128, self.NSLOT, 32, 128], BF16) as WR,
            nc.sbuf_tensor("OB", [128, self.NOB, 1152], F32) as OB,
            nc.sbuf_tensor("CST", [128, NCST], F32) as CST,
            nc.sbuf_tensor("PRM", [128, 2, NPRM], F32) as PRM,
            nc.sbuf_tensor("MODS", [128, 2, 192], F32) as MOD,
            nc.sbuf_tensor("GSS", [128, 2, 2, 32], F32) as GS,
            nc.sbuf_tensor("ONES", [128, 128], F32) as ONES,
            nc.sbuf_tensor("PSWB", [128, 128], BF16) as PSWB,
            nc.sbuf_tensor("CTB", [128, 32], BF16) as CTB,
            nc.sbuf_tensor("QOT", [128, 2, TS], BF16) as QOT,
            nc.psum_tensor("ps", [128, 8, 512], F32) as ps,
        ):
            self.HB, self.R2, self.WR, self.OB, self.CST, self.PRM = HB, R2, WR, OB, CST, PRM
            self.MODS, self.GSS, self.ONES, self.PSWB, self.CTB, self.ps = MOD, GS, ONES, PSWB, CTB, ps
            self.QOT = QOT
            self.KVOFF = 15360 - 3840
            kv = Carve(R2, 15360)
            kv.off = self.KVOFF
            self.KT = kv.bf16(2 * TS).rearrange("p (c t) -> p c t", c=2)
            self.KS = kv.bf16(2 * TS).rearrange("p (c t) -> p c t", c=2)
            self.VT = kv.bf16(10 * 256).rearrange("p (b c) -> p b c", b=10)

            def program():
                S = self.S
                self.wcur = -1
                self.wnext = 0
                self.pending = []
                self.obi = 0
                self.adai = 0
                self.ada_q = [(l, u) for l in range(self.depth) for u in range(192)] if self.lvl >= 1 else []
                S.op('sp', lambda e: e.dma_start(out=CST[:, :], in_=self.cst[:, :]), writes=['cst'], dsem='cst')
                S.op('sp', lambda e: e.dma_start(out=PRM[:, :, :], in_=self.prm.rearrange("l p n -> p l n")), writes=['prm'], dsem='prm')
                S.op('dve', lambda e: e.memset(ONES[:, :], 1.0), writes=['ones'])
                S.op('dve', lambda e: e.tensor_copy(out=PSWB[:, :], in_=CST[:, 384:512]), reads=['cst'], writes=['pswb'])
                S.op('act', lambda e: e.activation(out=CTB[:, :], in_=CST[:, 778:810], func=AF.Silu), reads=['cst'], writes=['ctb'])
                plist = [('x0', lambda: self.phase_x0())]
                for l in range(self.depth):
                    plist += [(f'adaln{l}', lambda l=l: self.phase_adaln(l)),
                              (f'norm1_{l}', lambda l=l: self.phase_norm(l, 1)),
                              (f'inproj{l}', lambda l=l: self.phase_inproj(l)),
                              (f'attn{l}', lambda l=l: self.phase_attn(l)),
                              (f'pnorm{l}', lambda l=l: self.phase_pnorm(l)),
                              (f'outproj{l}', lambda l=l: self.phase_outproj(l)),
                              (f'norm2_{l}', lambda l=l: self.phase_norm(l, 2)),
                              (f'ffn{l}', lambda l=l: self.phase_ffn(l))]
                plist.append(('final', lambda: self.phase_final()))
                for name, fn in plist:
                    S.barrier()
                    fn()
                    if self.stop is not None and self.stop.startswith(name):
                        break
                self.flush_stores()
                S.barrier()
                if self.debug:
                    S.op('sp', lambda e: e.dma_start(out=self.dhb[:, :, :], in_=HB[:, :, :]), dsem='dbg1')
                    S.op('sp', lambda e: e.dma_start(out=self.dr2[:, :], in_=R2[:, :]), dsem='dbg2')
                    S.op('sp', lambda e: e.dma_start(out=self.dmod[:, :], in_=MOD[:, 0, :]), dsem='dbg3')
                S.wait_all('sp', [(k, v) for k, v in S.dcnt.items()])

            self.units = []
            self.dry = True
            self.S = DummySched()
            program()
            self.dry = False
            self.S = S = Sched(nc)
            program()
            S.emit()
        return nc

    def phase_x0(self):
        S, ps, CST = self.S, self.ps, self.CST
        cv = Carve(self.R2, 15360)
        XB = [cv.f32(4096) for _ in range(2)]
        XS = cv.f32(4096).rearrange("p (c t) -> p c t", c=32)
        xTv = self.xT.rearrange("(c p) t -> p c t", p=128)
        BM = [0, 1, 2, 3, 4, 5, 0, 1]
        for tb in range(10):
            self.ada_tick(7, (0, 64))
            xb = XB[tb % 2]
            S.op('sp', lambda e, xb=xb, tb=tb: e.dma_start(out=xb, in_=self.xin[tb * 128:(tb + 1) * 128, :]),
                 writes=[('xb', tb % 2)], dsem=f"xb{tb % 2}")
            for c in range(32):
                bk = BM[c // 4]
                S.op('pe', lambda e, xb=xb, c=c, bk=bk: e.transpose(
                    ps[:, bk, (c % 4) * 128:(c % 4 + 1) * 128], xb[:, c * 128:(c + 1) * 128], CST[:, 0:128]),
                    reads=[('xb', tb % 2), 'cst'], writes=[('ps', bk)], signal=(c % 4 == 3))
                if c % 4 == 3:
                    b = c // 4
                    eng = 'act' if bk % 2 == 0 else 'dve'
                    o_ = XS[:, b * 4:(b + 1) * 4, :]
                    i_ = ps[:, bk, :].rearrange("p (j t) -> p j t", j=4)
                    if eng == 'act':
                        S.op('act', lambda e, o_=o_, i_=i_: e.activation(out=o_, in_=i_, func=AF.Copy),
                             reads=[('ps', bk)], writes=[('xs', b)])
                    else:
                        S.op('dve', lambda e, o_=o_, i_=i_: e.tensor_copy(out=o_, in_=i_),
                             reads=[('ps', bk)], writes=[('xs', b)])
            S.op('sp', lambda e, tb=tb: e.dma_start(out=xTv[:, :, tb * 128:(tb + 1) * 128], in_=XS[:, :, :]),
                 reads=[('xs', b) for b in range(8)], writes=[('x', c) for c in range(32)], dsem="xsst")

    def phase_adaln(self, l):
        self.ada_until((l, 64) if l == 0 else (l, 192))
        self.gs_compute(l, 0)

    def gs_compute(self, l, which):
        S = self.S
        a, b_ = (32, 192) if which == 0 else (128, 224)
        S.op('dve', lambda e: e.scalar_tensor_tensor(out=self.GSS[:, l, which, :], in0=self.MODS[:, l, a:a + 32], scalar=1.0,
                                                     in1=self.PRM[:, l, b_:b_ + 32], op0=ALU.add, op1=ALU.mult),
             reads=[('mod', l, a // 32), 'prm'], writes=[('gs', l, which)])

    def phase_norm(self, l, which):
        S, ps, HB, CST = self.S, self.ps, self.HB, self.CST
        t0 = LA[l] if which == 1 else LB[l]
        gi = 0 if which == 1 else 1
        shoff = 0 if which == 1 else 96
        if which == 2:
            self.ada_until((l, 160))
            self.gs_compute(l, 1)
        MODl = self.MODS[:, l, :]
        GSl = self.GSS[:, l, :, :]
        cv = Carve(self.R2, self.KVOFF)
        XS = [cv.f32(512) for _ in range(4)]
        SQ = [cv.f32(512) for _ in range(2)]
        TM = [cv.f32(512) for _ in range(2)]
        RT = cv.f32(512)
        RS = cv.f32(512)
        ld = 0
        for (ts, tn) in tiles_from(t0):
            for c in range(32):
                xs = XS[ld % 4]
                S.op('sp', lambda e, xs=xs, c=c, ts=ts, tn=tn: e.dma_start(out=xs[:, 0:tn], in_=self.xT[c * 128:(c + 1) * 128, ts:ts + tn]),
                     reads=[('x', c)], writes=[('nxs', ld % 4)], dsem=f"nxs{ld % 4}")
                sq = SQ[c % 2]
                S.op('act', lambda e, xs=xs, sq=sq, tn=tn: e.activation(out=sq[:, 0:tn], in_=xs[:, 0:tn], func=AF.Square),
                     reads=[('nxs', ld % 4)], writes=[('nsq', c % 2)])
                S.op('pe', lambda e, sq=sq, tn=tn, c=c: e.matmul(ps[:, 6, 0:tn], lhsT=self.ONES[:, :], rhs=sq[:, 0:tn],
                                                                 start=(c == 0), stop=(c == 31)),
                     reads=[('nsq', c % 2), 'ones'], writes=[('ps', 6)])
                ld += 1
            S.op('act', lambda e, tn=tn: e.activation(out=RT[:, 0:tn], in_=ps[:, 6, 0:tn], func=AF.Sqrt,
                                                      bias=CST[:, 811:812], scale=1.0 / D),
                 reads=[('ps', 6), 'cst'], writes=['nrt'])
            S.op('dve', lambda e, tn=tn: e.reciprocal(out=RS[:, 0:tn], in_=RT[:, 0:tn]), reads=['nrt'], writes=['nrs'])
            for c in range(32):
                xs = XS[ld % 4]
                S.op('sp', lambda e, xs=xs, c=c, ts=ts, tn=tn: e.dma_start(out=xs[:, 0:tn], in_=self.xT[c * 128:(c + 1) * 128, ts:ts + tn]),
                     reads=[('x', c)], writes=[('nxs', ld % 4)], dsem=f"nxs{ld % 4}")
                tm = TM[c % 2]
                S.op('dve', lambda e, xs=xs, tm=tm, tn=tn: e.tensor_tensor(out=tm[:, 0:tn], in0=xs[:, 0:tn], in1=RS[:, 0:tn], op=ALU.mult),
                     reads=[('nxs', ld % 4), 'nrs'], writes=[('ntm', c % 2)])
                S.op('act', lambda e, tm=tm, c=c, ts=ts, tn=tn: e.activation(
                    out=HB[:, c, ts - t0:ts - t0 + tn], in_=tm[:, 0:tn], func=AF.Identity,
                    bias=MODl[:, shoff + c:shoff + c + 1], scale=GSl[:, gi, c:c + 1]),
                    reads=[('ntm', c % 2), ('mod', l, shoff // 32), ('gs', l, gi)], writes=[('hb', c)])
                ld += 1

    def phase_inproj(self, l):
        S, ps, HB, CST, PRM = self.S, self.ps, self.HB, self.CST, self.PRM[:, l, :]
        A, B = LA[l], LB[l]
        T1, T2 = TS - A, TS - B
        tl = tiles_from(A)
        wi = self.w_in[l].rearrange("(kt p) c -> p kt c", p=128)
        cv = Carve(self.R2, self.KVOFF)
        Ct = cv.f32(TS)
        St = cv.f32(TS)
        VR = cv.f32(TS)
        E = [cv.f32(TS) for _ in range(5)]
        QO = [self.QOT[:, 0, :], self.QOT[:, 1, :]]
        posi = E[0].bitcast(I32)
        S.op('sp', lambda e: e.dma_start(out=posi[:, 0:T1], in_=self.pos[0:1, A:TS].partition_broadcast(128)),
             writes=['e0'], dsem='posi')
        S.op('sp', lambda e: e.dma_start(out=VR[:, 0:T1], in_=self.valid[0:1, A:TS].partition_broadcast(128)),
             writes=['vr'], dsem='vr')
        S.op('dve', lambda e: e.tensor_copy(out=E[1][:, 0:T1], in_=posi[:, 0:T1]), reads=['e0'], writes=['e1'])
        S.op('dve', lambda e: e.tensor_scalar(out=E[2][:, 0:T1], in0=E[1][:, 0:T1], scalar1=CST[:, 810:811], scalar2=None, op0=ALU.mult),
             reads=['e1', 'cst'], writes=['e2'])
        for (dst, dkey, add) in ((St, 'st', 0.0), (Ct, 'ct', math.pi / 2)):
            S.op('dve', lambda e, add=add: e.tensor_scalar(out=E[3][:, 0:T1], in0=E[2][:, 0:T1], scalar1=1.0 / (2 * math.pi),
                                                           scalar2=add / (2 * math.pi), op0=ALU.mult, op1=ALU.add),
                 reads=['e2'], writes=['e3'])
            S.op('dve', lambda e: e.tensor_copy(out=posi[:, 0:T1], in_=E[3][:, 0:T1]), reads=['e3'], writes=['e0'])
            S.op('dve', lambda e: e.tensor_copy(out=E[4][:, 0:T1], in_=posi[:, 0:T1]), reads=['e0'], writes=['e4'])
            S.op('dve', lambda e: e.scalar_tensor_tensor(out=E[3][:, 0:T1], in0=E[4][:, 0:T1], scalar=-C1, in1=E[2][:, 0:T1],
                                                         op0=ALU.mult, op1=ALU.add), reads=['e4', 'e2', 'e3'], writes=['e3'])
            S.op('dve', lambda e: e.scalar_tensor_tensor(out=E[3][:, 0:T1], in0=E[4][:, 0:T1], scalar=-C2, in1=E[3][:, 0:T1],
                                                         op0=ALU.mult, op1=ALU.add), reads=['e4', 'e3'], writes=['e3'])
            S.op('dve', lambda e, add=add: e.tensor_scalar(out=E[3][:, 0:T1], in0=E[3][:, 0:T1], scalar1=add, scalar2=PI_LO,
                                                           op0=ALU.add, op1=ALU.min), reads=['e3'], writes=['e3'])
            S.op('dve', lambda e: e.tensor_scalar(out=E[3][:, 0:T1], in0=E[3][:, 0:T1], scalar1=-PI_LO, scalar2=None, op0=ALU.max),
                 reads=['e3'], writes=['e3'])
            S.op('act', lambda e, dst=dst: e.activation(out=dst[:, 0:T1], in_=E[3][:, 0:T1], func=AF.Sin),
                 reads=['e3'], writes=[dkey])
        S.barrier()
        if self.stop == 'inproj0a':
            return

        def rhs_h(kt, ti):
            ts, tn = tl[ti]
            return HB[:, kt, ts - A:ts - A + tn]

        def rk_h(kt, ti):
            return [('hb', kt)]

        slots = {}

        def qk_main(ci):
            slots[ci] = self.acquire(wi[:, :, ci * 128:(ci + 1) * 128], 32)
            self.main_mm(slots[ci], 32, ci % 2, tl, rhs_h, rk_h)

        import os as _os2
        QKE = int(_os2.environ.get('QKE', '99'))

        def qk_epi(ci):
            pset = ci % 2
            stepc = [0]
            realop = S.op

            def gop(*a, **k):
                stepc[0] += 1
                if stepc[0] <= QKE:
                    return realop(*a, **k)
            class _G:
                op = staticmethod(gop)
            S_ = _G
            gcol = 256 if ci < 16 else 257
            isk = ci >= 16
            qo = self.KT[:, ci - 16, :] if isk else QO[ci % 2]
            qokey = ('kt', ci - 16) if isk else ('qo', ci % 2)
            for ti, (ts, tn) in enumerate(tl):
                stepc[0] = 0
                b = pset * 3 + ti
                lo = ts - A
                sl = slice(lo, lo + tn)
                k = f"{ti}"
                S_.op('act', lambda e, b=b, sl=sl, tn=tn: e.activation(out=E[0][:, sl], in_=ps[:, b, 0:tn], func=AF.Square),
                     reads=[('ps', b)], writes=['qe0' + k])
                S_.op('dve', lambda e, b=b, sl=sl, tn=tn: e.tensor_scalar(out=E[1][:, sl], in0=ps[:, b, 0:tn], scalar1=PRM[:, gcol:gcol + 1],
                                                                         scalar2=None, op0=ALU.mult),
                     reads=[('ps', b), 'prm', 'qe0' + k], writes=['qe1' + k])
                ab = 6 + (ti % 2)
                S_.op('pe', lambda e, ab=ab, sl=sl, tn=tn: e.matmul(ps[:, ab, 0:tn], lhsT=CST[:, 128:256], rhs=E[0][:, sl], start=True, stop=True),
                     reads=['qe0' + k, 'cst'], writes=[('ps', ab)])
                S_.op('act', lambda e, ab=ab, sl=sl, tn=tn: e.activation(out=E[2][:, sl], in_=ps[:, ab, 0:tn], func=AF.Sqrt,
                                                                        bias=CST[:, 811:812], scale=1.0 / 64),
                     reads=[('ps', ab), 'cst'], writes=['qe2' + k])
                S_.op('dve', lambda e, sl=sl: e.reciprocal(out=E[2][:, sl], in_=E[2][:, sl]), reads=['qe2' + k], writes=['qe2' + k])
                S_.op('dve', lambda e, sl=sl: e.tensor_tensor(out=E[1][:, sl], in0=E[1][:, sl], in1=E[2][:, sl], op=ALU.mult),
                     reads=['qe1' + k, 'qe2' + k], writes=['qe1' + k])
                ab2 = 6 + ((ti + 1) % 2)
                S_.op('pe', lambda e, ab2=ab2, sl=sl, tn=tn: e.matmul(ps[:, ab2, 0:tn], lhsT=CST[:, 256:384], rhs=E[1][:, sl], start=True, stop=True),
                     reads=['qe1' + k, 'cst'], writes=[('ps', ab2)])
                S_.op('dve', lambda e, sl=sl: e.tensor_tensor(out=E[0][:, sl], in0=E[1][:, sl], in1=Ct[:, sl], op=ALU.mult),
                     reads=['qe1' + k, 'ct', 'qe0' + k], writes=['qe0' + k])
                S_.op('dve', lambda e, ab2=ab2, sl=sl, tn=tn: e.tensor_tensor(out=E[2][:, sl], in0=ps[:, ab2, 0:tn], in1=St[:, sl], op=ALU.mult),
                     reads=[('ps', ab2), 'st', 'qe2' + k], writes=['qe2' + k])
                S_.op('dve', lambda e, sl=sl, qo=qo: e.tensor_tensor(out=qo[:, sl], in0=E[0][:, sl], in1=E[2][:, sl], op=ALU.add),
                     reads=['qe0' + k, 'qe2' + k], writes=[qokey])
                if isk:
                    ab3 = 6 + (ti % 2)
                    S_.op('pe', lambda e, ab3=ab3, sl=sl, tn=tn, qo=qo: e.matmul(ps[:, ab3, 0:tn], lhsT=self.PSWB[:, :], rhs=qo[:, sl], start=True, stop=True),
                         reads=[qokey, 'pswb'], writes=[('ps', ab3)])
                    S_.op('act', lambda e, ab3=ab3, sl=sl, tn=tn: e.activation(out=self.KS[:, ci - 16, sl], in_=ps[:, ab3, 0:tn], func=AF.Copy),
                         reads=[('ps', ab3)], writes=[('ks', ci - 16)])
            if not isk:
                S_.op('sp', lambda e, qo=qo: e.dma_start(out=self.qscr[ci][:, 0:T1], in_=qo[:, 0:T1]),
                     reads=[qokey], writes=[('qscr', ci)], dsem=f"qo{ci % 2}")

        import os as _os
        QKN = int(_os.environ.get('QKN', '18'))
        qk_main(0)
        for ci in range(QKN):
            if ci + 1 < QKN:
                qk_main(ci + 1)
            qk_epi(ci)
        if QKN < 18:
            return

        if self.stop == 'inproj0b':
            return
        nb = T1 // 128
        for vu in range(2):
            slot = self.acquire(wi[:, :, (18 + vu) * 128:(19 + vu) * 128], 32)
            pset = vu % 2
            for tb in range(nb):
                b = pset * 3 + tb // 4
                for kt in range(32):
                    S.op('pe', lambda e, slot=slot, tb=tb, kt=kt, b=b: e.matmul(
                        ps[:, b, (tb % 4) * 128:(tb % 4 + 1) * 128], lhsT=HB[:, kt, tb * 128:(tb + 1) * 128],
                        rhs=self.WR[:, slot, kt, :], start=(kt == 0), stop=(kt == 31)),
                        reads=[('w', slot, kt // 16), ('hb', kt)], writes=[('ps', b)], signal=(kt == 31))
            for bi in range((nb + 3) // 4):
                b = pset * 3 + bi
                n = min(4, nb - bi * 4)
                blk0 = A // 128 + bi * 4
                S.op('act', lambda e, b=b, n=n, blk0=blk0, vu=vu: e.activation(
                    out=self.VT[:, blk0:blk0 + n, vu * 128:(vu + 1) * 128],
                    in_=ps[:, b, 0:n * 128].rearrange("p (j t) -> p j t", j=n), func=AF.Copy),
                    reads=[('ps', b)], writes=[('vt', vu)])
        S.barrier()
        if self.stop == 'inproj0c':
            return

        cv2 = Carve(self.R2, self.KVOFF)
        cv2.off = 3 * TS
        ACP = cv2.f32(TS)
        SGM = cv2.f32(TS)
        HG = cv2.f32(TS + 32)
        ACC = [cv2.f32(1152) for _ in range(2)]
        S.op('dve', lambda e: e.memset(HG[:, 0:32], 0.0), writes=['hg'])
        off = B - A - 30 + 30
        for i in range(16):
            sa = self.acquire(wi[:, :, (20 + i) * 128:(21 + i) * 128], 32)
            self.main_mm(sa, 32, 0, tl, rhs_h, rk_h)
            sg_ = self.acquire(wi[:, :, (36 + i) * 128:(37 + i) * 128], 32)
            self.main_mm(sg_, 32, 1, tl, rhs_h, rk_h)
            if l == 0:
                self.ada_tick(2, (0, 96))
            for ti, (ts, tn) in enumerate(tl):
                sl = slice(ts - A, ts - A + tn)
                S.op('act', lambda e, ti=ti, sl=sl, tn=tn: e.activation(out=ACP[:, sl], in_=ps[:, ti, 0:tn], func=AF.Copy),
                     reads=[('ps', ti)], writes=['acp'])
            for ti, (ts, tn) in enumerate(tl):
                sl = slice(ts - A, ts - A + tn)
                S.op('act', lambda e, ti=ti, sl=sl, tn=tn: e.activation(out=SGM[:, sl], in_=ps[:, 3 + ti, 0:tn], func=AF.Sigmoid),
                     reads=[('ps', 3 + ti)], writes=['sgm'])
            S.op('dve', lambda e: e.tensor_tensor(out=SGM[:, 0:T1], in0=SGM[:, 0:T1], in1=VR[:, 0:T1], op=ALU.mult),
                 reads=['sgm', 'vr'], writes=['sgm'])
            S.op('dve', lambda e: e.tensor_tensor(out=HG[:, 30:30 + T1], in0=ACP[:, 0:T1], in1=SGM[:, 0:T1], op=ALU.mult),
                 reads=['sgm', 'acp', 'hg'], writes=['hg'])
            acc = ACC[i % 2]
            wb = 354 + i * 31
            S.op('dve', lambda e, acc=acc, wb=wb, i=i: e.tensor_scalar(
                out=acc[:, 0:T2], in0=HG[:, off:off + T2], scalar1=PRM[:, wb:wb + 1], scalar2=PRM[:, 274 + i:275 + i],
                op0=ALU.mult, op1=ALU.add), reads=['hg', 'prm'], writes=[('acc', i % 2)])
            for j in range(1, 31):
                S.op('dve', lambda e, acc=acc, wb=wb, j=j: e.scalar_tensor_tensor(
                    out=acc[:, 0:T2], in0=HG[:, off + j:off + j + T2], scalar=PRM[:, wb + j:wb + j + 1], in1=acc[:, 0:T2],
                    op0=ALU.mult, op1=ALU.add), reads=['hg', 'prm', ('acc', i % 2)], writes=[('acc', i % 2)])
            S.op('sp', lambda e, acc=acc, i=i: e.dma_start(out=self.mscr[(16 + i) * 128:(17 + i) * 128, 0:T2], in_=acc[:, 0:T2]),
                 reads=[('acc', i % 2)], writes=[('ms', 16 + i)], dsem=f"acc{i % 2}")

    def phase_attn(self, l):
        S, ps, CST, PRM = self.S, self.ps, self.CST, self.PRM[:, l, :]
        A, B = LA[l], LB[l]
        T1, T2 = TS - A, TS - B
        hbf = self.HB[:, :, :].rearrange("p c t -> p (c t)")
        o = 0
        VP = hbf[:, o:o + 10240].rearrange("p (b g e c) -> p b g e c", b=10, g=4, e=2)
        o += 10240
        QG = []
        for _ in range(2):
            QG.append(hbf[:, o:o + 4 * TS].rearrange("p (c t) -> p c t", c=4))
            o += 4 * TS
        EB = []
        PT = []
        for _ in range(2):
            EB.append(hbf[:, o:o + 2048].rearrange("p (b t) -> p b t", b=4))
            o += 2048
        for _ in range(2):
            PT.append(hbf[:, o:o + 2048].rearrange("p (b t) -> p b t", b=4))
            o += 2048
        MK = hbf[:, o:o + 1024].rearrange("p (m t) -> p m t", m=2)
        o += 1024
        OP = hbf[:, o:o + 256].rearrange("p (e c) -> p e c", e=2)
        o += 256
        cv = Carve(self.R2, self.KVOFF)
        DEN = [cv.f32(512) for _ in range(2)]
        AO = [cv.f32(512) for _ in range(2)]
        ESK = cv.f32(16)
        S.op('dve', lambda e: e.memset(VP.rearrange("p b g e c -> p (b g e c)"), 0.0), writes=['vp'])
        S.op('dve', lambda e: e.memset(OP.rearrange("p e c -> p (e c)"), 0.0), writes=['op'])
        for e_ in range(2):
            S.op('dve', lambda e, e_=e_: e.memset(OP[:, e_, e_ * 64:(e_ + 1) * 64], 1.0), reads=['op'], writes=['op'])
            for g in range(4):
                S.op('dve', lambda e, e_=e_, g=g: e.tensor_copy(out=VP[:, :, g, e_, e_ * 64:(e_ + 1) * 64], in_=self.VT[:, :, g * 64:(g + 1) * 64]),
                     reads=['vp', ('vt', 0), ('vt', 1)], writes=['vp'])
        for m in range(2):
            for j in range(4):
                S.op('dve', lambda e, m=m, j=j: e.tensor_copy(out=MK[:, m, j * 128:(j + 1) * 128], in_=CST[:, 512 + m * 128:640 + m * 128]),
                     reads=['cst'], writes=['mk'])
        S.op('act', lambda e: e.activation(out=ESK[:, :], in_=PRM[:, 258:274], func=AF.Exp), reads=['prm'], writes=['esk'])
        mview = self.mscr.rearrange("(c p) t -> p c t", p=128)
        it = 0
        for g in range(4):
            qg = QG[g % 2]
            S.op('sp', lambda e, qg=qg, g=g: e.dma_start(out=qg[:, :, 0:T1], in_=self.qscr[4 * g:4 * g + 4, :, 0:T1].rearrange("c p t -> p c t")),
                 reads=[('qscr', 4 * g + j) for j in range(4)], writes=[('qg', g % 2)], dsem=f"qg{g % 2}")
            c = g // 2
            for qb in range(B // 128, 10):
                eb = EB[it % 2]
                pt = PT[it % 2]
                qcol = qb * 128 - A
                bis = []
                for e_ in range(2):
                    ksrc, kkey = (self.KT, ('kt', c)) if e_ == (g % 2) else (self.KS, ('ks', c))
                    for kk, kb in enumerate((qb - 1, qb)):
                        bi = e_ * 2 + kk
                        bis.append((bi, e_, kk, kb))
                        kcol = kb * 128 - A
                        S.op('pe', lambda e, bi=bi, e_=e_, ksrc=ksrc, kcol=kcol, qg=qg, qcol=qcol, c=c: e.matmul(
                            ps[:, bi, :], lhsT=ksrc[e_ * 64:(e_ + 1) * 64, c, kcol:kcol + 128],
                            rhs=qg[e_ * 64:(e_ + 1) * 64, :, qcol:qcol + 128], start=True, stop=True),
                            reads=[kkey, ('qg', g % 2)], writes=[('ps', bi)])
                for (bi, e_, kk, kb) in bis:
                    S.op('act', lambda e, bi=bi, eb=eb: e.activation(out=eb[:, bi, :], in_=ps[:, bi, :], func=AF.Exp, scale=0.125),
                         reads=[('ps', bi)], writes=[('eb', it % 2, bi)])
                    S.op('dve', lambda e, bi=bi, eb=eb, pt=pt, kk=kk, kb=kb: e.scalar_tensor_tensor(
                        out=pt[:, bi, :], in0=eb[:, bi, :], scalar=CST[:, 768 + kb:769 + kb], in1=MK[:, 1 - kk, :],
                        op0=ALU.mult, op1=ALU.mult), reads=[('eb', it % 2, bi), 'cst', 'mk'], writes=[('pt', it % 2, bi)])
                bo = 4 + (it % 2)
                bd = 6 + (it % 2)
                for n_, (bi, e_, kk, kb) in enumerate(bis):
                    S.op('pe', lambda e, bi=bi, e_=e_, kb=kb, pt=pt, n_=n_, bo=bo, g=g: e.matmul(
                        ps[:, bo, :], lhsT=VP[:, kb, g, e_, :], rhs=pt[:, bi, :], start=(n_ == 0), stop=(n_ == 3)),
                        reads=[('pt', it % 2, bi), 'vp'], writes=[('ps', bo)])
                for n_, (bi, e_, kk, kb) in enumerate(bis):
                    S.op('pe', lambda e, bi=bi, e_=e_, pt=pt, n_=n_, bd=bd: e.matmul(
                        ps[:, bd, :], lhsT=OP[:, e_, :], rhs=pt[:, bi, :], start=(n_ == 0), stop=(n_ == 3)),
                        reads=[('pt', it % 2, bi), 'op'], writes=[('ps', bd)])
                den = DEN[it % 2]
                ao = AO[it % 2]
                for j in range(4):
                    S.op('dve', lambda e, j=j, den=den, bd=bd, g=g: e.tensor_scalar(
                        out=den[:, j * 128:(j + 1) * 128], in0=ps[:, bd, j * 128:(j + 1) * 128],
                        scalar1=ESK[:, 4 * g + j:4 * g + j + 1], scalar2=None, op0=ALU.add),
                        reads=[('ps', bd), 'esk'], writes=[('den', it % 2)])
                S.op('dve', lambda e, den=den: e.reciprocal(out=den[:, :], in_=den[:, :]), reads=[('den', it % 2)], writes=[('den', it % 2)])
                S.op('dve', lambda e, den=den, ao=ao, bo=bo: e.tensor_tensor(out=ao[:, :], in0=ps[:, bo, :], in1=den[:, :], op=ALU.mult),
                     reads=[('ps', bo), ('den', it % 2)], writes=[('ao', it % 2)])
                ocol = qb * 128 - B
                S.op('sp', lambda e, ao=ao, ocol=ocol, g=g: e.dma_start(
                    out=mview[:, 4 * g:4 * g + 4, ocol:ocol + 128], in_=ao.rearrange("p (j t) -> p j t", j=4)),
                    reads=[('ao', it % 2)], writes=[('ms', 4 * g + j) for j in range(4)], dsem=f"ao{it % 2}")
                it += 1

    def phase_pnorm(self, l):
        S, ps, HB, CST, PRM = self.S, self.ps, self.HB, self.CST, self.PRM[:, l, :]
        B = LB[l]
        cv = Carve(self.R2, self.KVOFF)
        CB = cv.f32(16 * 512).rearrange("p (c t) -> p c t", c=16)
        SQ = [cv.f32(512) for _ in range(2)]
        RT = cv.f32(512)
        RS = cv.f32(512)
        MU = cv.f32(512)
        NM = cv.f32(512)
        mview = self.mscr.rearrange("(c p) t -> p c t", p=128)
        cbk = [('cb', c) for c in range(16)]

        def rstd_from(bank, tn, eps_col, n):
            S.op('act', lambda e: e.activation(out=RT[:, 0:tn], in_=ps[:, bank, 0:tn], func=AF.Sqrt,
                                               bias=CST[:, eps_col:eps_col + 1], scale=1.0 / n),
                 reads=[('ps', bank), 'cst'], writes=['prt'])
            S.op('dve', lambda e: e.reciprocal(out=RS[:, 0:tn], in_=RT[:, 0:tn]), reads=['prt'], writes=['prs'])

        def sq_acc(c, tn, bank):
            sq = SQ[c % 2]
            S.op('act', lambda e, sq=sq, c=c, tn=tn: e.activation(out=sq[:, 0:tn], in_=CB[:, c, 0:tn], func=AF.Square),
                 reads=[('cb', c)], writes=[('psq', c % 2)])
            S.op('pe', lambda e, sq=sq, c=c, tn=tn, bank=bank: e.matmul(ps[:, bank, 0:tn], lhsT=self.ONES[:, :], rhs=sq[:, 0:tn],
                                                                        start=(c == 0), stop=(c == 15)),
                 reads=[('psq', c % 2), 'ones'], writes=[('ps', bank)])

        for (ts, tn) in tiles_from(B):
            o0 = ts - B
            S.op('sp', lambda e, tn=tn, o0=o0: e.dma_start(out=CB[:, :, 0:tn], in_=mview[:, 0:16, o0:o0 + tn]),
                 reads=[('ms', c) for c in range(16)], writes=cbk, dsem="pcb")
            for c in range(16):
                sq_acc(c, tn, 6)
            rstd_from(6, tn, 811, 2048)
            for c in range(16):
                S.op('dve', lambda e, c=c, tn=tn, o0=o0: e.scalar_tensor_tensor(
                    out=HB[:, c, o0:o0 + tn], in0=CB[:, c, 0:tn], scalar=PRM[:, 322 + c:323 + c], in1=RS[:, 0:tn],
                    op0=ALU.mult, op1=ALU.mult), reads=[('cb', c), 'prs', 'prm'], writes=[('hb', c)])
            S.op('sp', lambda e, tn=tn, o0=o0: e.dma_start(out=CB[:, :, 0:tn], in_=mview[:, 16:32, o0:o0 + tn]),
                 reads=[('ms', 16 + c) for c in range(16)], writes=cbk, dsem="pcb")
            for c in range(16):
                S.op('pe', lambda e, c=c, tn=tn: e.matmul(ps[:, 7, 0:tn], lhsT=self.ONES[:, :], rhs=CB[:, c, 0:tn], start=(c == 0), stop=(c == 15)),
                     reads=[('cb', c), 'ones'], writes=[('ps', 7)])
                sq_acc(c, tn, 6)
            S.op('dve', lambda e, tn=tn: e.tensor_scalar(out=MU[:, 0:tn], in0=ps[:, 7, 0:tn], scalar1=1.0 / 2048, scalar2=None, op0=ALU.mult),
                 reads=[('ps', 7)], writes=['pmu'])
            S.op('dve', lambda e, tn=tn: e.tensor_tensor(out=NM[:, 0:tn], in0=MU[:, 0:tn], in1=MU[:, 0:tn], op=ALU.mult),
                 reads=['pmu'], writes=['pnm'])
            S.op('dve', lambda e, tn=tn: e.scalar_tensor_tensor(out=NM[:, 0:tn], in0=ps[:, 6, 0:tn], scalar=1.0 / 2048, in1=NM[:, 0:tn],
                                                                op0=ALU.mult, op1=ALU.subtract), reads=[('ps', 6), 'pnm'], writes=['pnm'])
            S.op('act', lambda e, tn=tn: e.activation(out=RT[:, 0:tn], in_=NM[:, 0:tn], func=AF.Sqrt, bias=CST[:, 812:813], scale=1.0),
                 reads=['pnm', 'cst'], writes=['prt'])
            S.op('dve', lambda e, tn=tn: e.reciprocal(out=RS[:, 0:tn], in_=RT[:, 0:tn]), reads=['prt'], writes=['prs'])
            S.op('dve', lambda e, tn=tn: e.scalar_tensor_tensor(out=NM[:, 0:tn], in0=MU[:, 0:tn], scalar=-1.0, in1=RS[:, 0:tn],
                                                                op0=ALU.mult, op1=ALU.mult), reads=['pmu', 'prs', 'pnm'], writes=['pnm'])
            for c in range(16):
                S.op('dve', lambda e, c=c, tn=tn: e.tensor_tensor(out=CB[:, c, 0:tn], in0=CB[:, c, 0:tn], in1=RS[:, 0:tn], op=ALU.mult),
                     reads=[('cb', c), 'prs'], writes=[('cb', c)])
                S.op('dve', lambda e, c=c, tn=tn: e.tensor_tensor(out=CB[:, c, 0:tn], in0=CB[:, c, 0:tn], in1=NM[:, 0:tn], op=ALU.add),
                     reads=[('cb', c), 'pnm'], writes=[('cb', c)])
                S.op('act', lambda e, c=c, tn=tn: e.activation(out=CB[:, c, 0:tn], in_=CB[:, c, 0:tn], func=AF.Silu,
                                                               bias=PRM[:, 306 + c:307 + c], scale=PRM[:, 290 + c:291 + c]),
                     reads=[('cb', c), 'prm'], writes=[('cb', c)])
                sq_acc(c, tn, 6)
            rstd_from(6, tn, 811, 2048)
            for c in range(16):
                S.op('dve', lambda e, c=c, tn=tn, o0=o0: e.scalar_tensor_tensor(
                    out=HB[:, 16 + c, o0:o0 + tn], in0=CB[:, c, 0:tn], scalar=PRM[:, 338 + c:339 + c], in1=RS[:, 0:tn],
                    op0=ALU.mult, op1=ALU.mult), reads=[('cb', c), 'prs', 'prm'], writes=[('hb', 16 + c)])

    def resid_epi(self, l, pset, tl, t0, gate_off, oc, obi):
        S, ps = self.S, self.ps
        obslot = obi % self.NOB
        for ti, (ts, tn) in enumerate(tl):
            b = pset * 3 + ti
            o_ = self.OB[:, obslot, ts - t0:ts - t0 + tn]
            if ti % 2 == 0:
                S.op('act', lambda e, b=b, o_=o_, tn=tn: e.activation(out=o_, in_=ps[:, b, 0:tn], func=AF.Identity,
                                                                      scale=self.MODS[:, l, gate_off + oc:gate_off + oc + 1]),
                     reads=[('ps', b), ('mod', l, gate_off // 32), ('ob', obslot)], writes=[('ob', obslot)])
            else:
                S.op('dve', lambda e, b=b, o_=o_, tn=tn: e.tensor_scalar(out=o_, in0=ps[:, b, 0:tn],
                                                                         scalar1=self.MODS[:, l, gate_off + oc:gate_off + oc + 1],
                                                                         scalar2=None, op0=ALU.mult),
                     reads=[('ps', b), ('mod', l, gate_off // 32), ('ob', obslot)], writes=[('ob', obslot)])
        self.x_accum(obslot, oc, t0)

    def phase_outproj(self, l):
        HB = self.HB
        B = LB[l]
        tl = tiles_from(B)

        def rhs(kt, ti):
            ts, tn = tl[ti]
            return HB[:, kt, ts - B:ts - B + tn]
        wo = self.w_out[l].rearrange("(kt p) c -> p kt c", p=128)
        self.ada_until((l, 96))
        for oc in range(32):
            slot = self.acquire(wo[:, :, oc * 128:(oc + 1) * 128], 32)
            self.main_mm(slot, 32, oc % 2, tl, rhs, lambda kt, ti: [('hb', kt)])
            if l == 0:
                self.ada_tick(2, (0, 160))
            self.resid_epi(l, oc % 2, tl, B, 64, oc, self.obi)
            self.obi += 1
        self.flush_stores()

    def phase_ffn(self, l):
        S, ps, HB = self.S, self.ps, self.HB
        B = LB[l]
        T2 = TS - B
        tl = tiles_from(B)
        cv = Carve(self.R2, 15360)
        ACTT = cv.bf16(16 * 1152).rearrange("p (j t) -> p j t", j=16)
        SG = [cv.f32(1152) for _ in range(2)]

        def rhs_h(kt, ti):
            ts, tn = tl[ti]
            return HB[:, kt, ts - B:ts - B + tn]
        pi = 0
        wg = self.w_g[l].rearrange("(kt p) c -> p kt c", p=128)
        wu = self.w_u[l].rearrange("(kt p) c -> p kt c", p=128)
        wd = self.w_d[l].rearrange("(kt p) c -> p kt c", p=128)

        def tick():
            if l == 0:
                if self.ada_q and self.ada_q[0] < (0, 192):
                    self.ada_tick(2, (0, 192))
                elif self.depth > 1:
                    self.ada_tick(1, (1, 192))
        for (f0, nf) in self.SC:
            for j in range(nf):
                f = f0 + j
                sg_slot = self.acquire(wg[:, :, f * 128:(f + 1) * 128], 32)
                self.main_mm(sg_slot, 32, 0, tl, rhs_h, lambda kt, ti: [('hb', kt)])
                tick()
                su_slot = self.acquire(wu[:, :, f * 128:(f + 1) * 128], 32)
                self.main_mm(su_slot, 32, 1, tl, rhs_h, lambda kt, ti: [('hb', kt)])
                tick()
                sg = SG[pi % 2]
                for ti, (ts, tn) in enumerate(tl):
                    sl = slice(ts - B, ts - B + tn)
                    S.op('act', lambda e, ti=ti, sl=sl, tn=tn, sg=sg: e.activation(out=sg[:, sl], in_=ps[:, ti, 0:tn], func=AF.Silu),
                         reads=[('ps', ti)], writes=[('sg', pi % 2, ti)])
                    S.op('dve', lambda e, ti=ti, sl=sl, tn=tn, sg=sg, j=j: e.tensor_tensor(out=ACTT[:, j, sl], in0=sg[:, sl], in1=ps[:, 3 + ti, 0:tn], op=ALU.mult),
                         reads=[('ps', 3 + ti), ('sg', pi % 2, ti)], writes=[('actt', j)])
                pi += 1

            def rhs_a(kt, ti):
                ts, tn = tl[ti]
                return ACTT[:, kt, ts - B:ts - B + tn]
            self.ada_until((l, 192))
            for oc in range(32):
                slot = self.acquire(wd[:, f0:f0 + nf, oc * 128:(oc + 1) * 128], nf)
                self.main_mm(slot, nf, oc % 2, tl, rhs_a, lambda kt, ti: [('actt', kt)])
                tick()
                self.resid_epi(l, oc % 2, tl, B, 160, oc, self.obi)
                self.obi += 1
        self.flush_stores()

    def phase_final(self):
        S, ps, CST = self.S, self.ps, self.CST
        cv = Carve(self.R2, 15360)
        XS = [cv.f32(4096).rearrange("p (c t) -> p c t", c=32) for _ in range(2)]
        XO = cv.f32(4096)
        xTv = self.xT.rearrange("(c p) t -> p c t", p=128)
        for tb in range(8):
            xs = XS[tb % 2]
            s0 = 256 + tb * 128
            S.op('sp', lambda e, xs=xs, s0=s0: e.dma_start(out=xs[:, :, :], in_=xTv[:, :, s0:s0 + 128]),
                 reads=[('x', c) for c in range(32)], writes=[('fxs', tb % 2)], dsem=f"fxs{tb % 2}")
            for c in range(32):
                S.op('pe', lambda e, xs=xs, c=c: e.transpose(
                    ps[:, c // 4, (c % 4) * 128:(c % 4 + 1) * 128], xs[:, c, :], CST[:, 0:128]),
                    reads=[('fxs', tb % 2), 'cst'], writes=[('ps', c // 4)], signal=(c % 4 == 3))
            for b in range(8):
                o_ = XO[:, b * 512:(b + 1) * 512]
                if b % 2 == 0:
                    S.op('act', lambda e, o_=o_, b=b: e.activation(out=o_, in_=ps[:, b, :], func=AF.Copy),
                         reads=[('ps', b), 'fxo'], writes=[('fxo', b)])
                else:
                    S.op('dve', lambda e, o_=o_, b=b: e.tensor_copy(out=o_, in_=ps[:, b, :]),
                         reads=[('ps', b), 'fxo'], writes=[('fxo', b)])
            S.op('sp', lambda e, tb=tb: e.dma_start(out=self.out[tb * 128:(tb + 1) * 128, :], in_=XO),
                 reads=[('fxo', b) for b in range(8)], writes=['fxo'], dsem="fout")
        k = S.dma_sem("fout")
        S.wait_all('sp', [(k, S.dcnt[k])])


def _consts():
    c = np.zeros((128, NCST), np.float32)
    c[:, 0:128] = np.eye(128, dtype=np.float32)
    p = np.arange(128)
    c[:, 128:256] = (p[:, None] // 64 == p[None, :] // 64).astype(np.float32)
    prot = np.zeros((128, 128), np.float32)
    for m in range(128):
        d = m % 64
        if d < 8:
            prot[m + 8, m] = -1.0
        elif d < 16:
            prot[m - 8, m] = 1.0
    c[:, 256:384] = prot
    psw = np.zeros((128, 128), np.float32)
    for m in range(128):
        psw[(m + 64) % 128, m] = 1.0
    c[:, 384:512] = psw
    c[:, 512:640] = (p[:, None] <= p[None, :]).astype(np.float32)
    c[:, 640:768] = (p[:, None] > p[None, :]).astype(np.float32)
    invf = (np.float32(500000.0) ** (-(np.arange(0, 16, 2, dtype=np.float32)) / np.float32(16))).astype(np.float32)
    for q in range(128):
        d = q % 64
        c[q, 810] = invf[d % 8] if d < 16 else 0.0
    c[:, 811] = 1e-6
    c[:, 812] = 1e-5
    return c


def _params(inp):
    prm = np.zeros((2, 128, NPRM), np.float32)

    def cl(v, n):
        return np.ascontiguousarray(np.asarray(v, np.float32).reshape(n, 128).T)
    for l in range(2):
        prm[l, :, 0:192] = cl(inp['b_ada'][l], 192)
        prm[l, :, 192:224] = cl(inp['norm1_g'][l], 32)
        prm[l, :, 224:256] = cl(inp['norm2_g'][l], 32)
        prm[l, :, 256] = np.tile(np.asarray(inp['q_norm_g'][l], np.float32), 2)
        prm[l, :, 257] = np.tile(np.asarray(inp['k_norm_g'][l], np.float32), 2)
        sk = np.asarray(inp['sinks'][l], np.float32).reshape(16, 2)
        prm[l, 0:64, 258:274] = sk[None, :, 0]
        prm[l, 64:128, 258:274] = sk[None, :, 1]
        prm[l, :, 274:290] = cl(inp['conv_b'][l], 16)
        prm[l, :, 290:306] = cl(inp['conv_ln_g'][l], 16)
        prm[l, :, 306:322] = cl(inp['conv_ln_b'][l], 16)
        prm[l, :, 322:338] = cl(inp['attn_out_g'][l], 16)
        prm[l, :, 338:354] = cl(inp['conv_out_g'][l], 16)
        cw = np.asarray(inp['conv_w'][l], np.float32).reshape(31, 16, 128)
        prm[l, :, 354:850] = np.ascontiguousarray(cw.transpose(2, 1, 0)).reshape(128, 496)
    return prm


_NC_CACHE = {}


def kernel(**inputs):
    inp = {k: np.asarray(v) for k, v in inputs.items()}
    x = inp['x'].astype(np.float32, copy=False)
    pos = inp['positions'].astype(np.int32, copy=False)
    if 'nc' not in _NC_CACHE:
        b = Builder()
        b.obi = 0
        _NC_CACHE['nc'] = b.build()
    nc = _NC_CACHE['nc']
    cbase = _consts()
    prm = _params(inp)
    shared = {k: np.ascontiguousarray(inp[k], dtype=np.float32) for k in
              ('w_ada', 'w_in', 'w_out', 'w_ffn_gate', 'w_ffn_up', 'w_ffn_down')}
    in_maps = []
    for r in range(8):
        b_, j = r // 4, r % 4
        s = 1024 * j
        xin = np.zeros((TS, D), np.float32)
        pp = np.zeros((1, TS), np.int32)
        vv = np.zeros((1, TS), np.float32)
        lo = max(0, s - 256)
        n = s + 1024 - lo
        xin[TS - n:] = x[b_, lo:s + 1024]
        pp[0, TS - n:] = pos[b_, lo:s + 1024]
        vv[0, TS - n:] = 1.0
        cst = cbase.copy()
        cst[:, 768:778] = vv.reshape(10, 128).T
        cst[:, 778:810] = inp['c'][b_].astype(np.float32).reshape(32, 128).T
        m = {'xin': xin, 'pos': pp, 'valid': vv, 'cst': cst, 'prm': prm}
        m.update(shared)
        in_maps.append(m)
    res = run_bass_kernel_spmd(nc, in_maps, core_ids=list(range(8)))
    out = np.empty((2, 4096, D), np.float32)
    for r in range(8):
        b_, j = r // 4, r % 4
        out[b_, 1024 * j:1024 * (j + 1)] = res.results[r]['out']
    return out
```
